# Optimizing a Trainium2 kernel written in Bass

```python
import jax, jax.numpy as jnp
from jax import lax
import numpy as np

D_MODEL = 1024
BATCH = 16
SEQ = 2048
DEPTH = 2

N_MIXERS = 2
N_CONV_LAYERS = (DEPTH + 1) // 2
N_MLA_LAYERS = DEPTH // 2
MIX_WIDTH = 2 * D_MODEL
MEM_LEN = 256
MEM_HEADS = 4
MEM_HEAD_DIM = 128
MEM_WIDTH = MEM_HEADS * MEM_HEAD_DIM
MAIN_WIDTH = MIX_WIDTH - MEM_WIDTH
CONV_WIDTH = MAIN_WIDTH
CONV_KERNEL = 31
MLA_HEADS = 12
MLA_NOPE = 128
MLA_ROPE = 64
MLA_V = 128
MLA_QK = MLA_NOPE + MLA_ROPE
Q_RANK = 512
KV_RANK = 256
ROPE_THETA = 10000.0
Q_BLOCK = 128
RMS_EPS = 1e-6
LN_EPS = 1e-5
CONV_IN_COLS = 2 * CONV_WIDTH + MEM_WIDTH + MIX_WIDTH
MLA_IN_COLS = Q_RANK + KV_RANK + MLA_ROPE + MEM_WIDTH + MIX_WIDTH

kernel_name = "hybrid_conformer_conv_mla_memxattn"


def rmsnorm(x, g):
    xf = x.astype(jnp.float32)
    y = xf * lax.rsqrt(jnp.mean(xf * xf, axis=-1, keepdims=True) + RMS_EPS)
    return (y * g.astype(jnp.float32)).astype(x.dtype)


def layernorm(x, g, b):
    xf = x.astype(jnp.float32)
    mu = jnp.mean(xf, axis=-1, keepdims=True)
    var = jnp.mean(jnp.square(xf - mu), axis=-1, keepdims=True)
    y = (xf - mu) * lax.rsqrt(var + LN_EPS)
    return (y * g.astype(jnp.float32) + b.astype(jnp.float32)).astype(x.dtype)


def rope_tables(positions):
    inv_freq = 1.0 / (ROPE_THETA ** (jnp.arange(0, MLA_ROPE, 2, dtype=jnp.float32) / MLA_ROPE))
    ang = positions.astype(jnp.float32)[..., None] * inv_freq
    return jnp.cos(ang), jnp.sin(ang)


def apply_rope(x, cos, sin):
    xf = x.astype(jnp.float32)
    x1, x2 = jnp.split(xf, 2, axis=-1)
    out = jnp.concatenate([x1 * cos - x2 * sin, x2 * cos + x1 * sin], axis=-1)
    return out.astype(x.dtype)


def mem_cross_attention(q, mem, mem_g, w_mem_kv):
    b, s = q.shape[0], q.shape[1]
    kv = rmsnorm(mem, mem_g) @ w_mem_kv
    k, v = jnp.split(kv, 2, axis=-1)
    k = k.reshape(b, -1, MEM_HEADS, MEM_HEAD_DIM)
    v = v.reshape(b, -1, MEM_HEADS, MEM_HEAD_DIM)
    qh = q.reshape(b, s, MEM_HEADS, MEM_HEAD_DIM)
    sc = jnp.einsum('bshd,bmhd->bhsm', qh, k).astype(jnp.float32) * (MEM_HEAD_DIM ** -0.5)
    p = jax.nn.softmax(sc, axis=-1).astype(v.dtype)
    o = jnp.einsum('bhsm,bmhd->bshd', p, v)
    return o.reshape(b, s, MEM_WIDTH)


def conv_branch(u, dw, dw_b, ln_g, ln_b):
    a, g = jnp.split(u, 2, axis=-1)
    h = a * jax.nn.sigmoid(g)
    h = lax.conv_general_dilated(
        h, dw[:, None, :], window_strides=(1,), padding=[(CONV_KERNEL - 1, 0)],
        dimension_numbers=('NWC', 'WIO', 'NWC'), feature_group_count=CONV_WIDTH) + dw_b
    h = layernorm(h, ln_g, ln_b)
    return jax.nn.silu(h)


def mla_branch(cq, ckv, kr, cos, sin, q_g, w_uq, kv_g, w_ukv):
    b, s = cq.shape[0], cq.shape[1]
    q = (rmsnorm(cq, q_g) @ w_uq).reshape(b, s, MLA_HEADS, MLA_QK)
    q_nope = q[..., :MLA_NOPE]
    q_rope = apply_rope(q[..., MLA_NOPE:], cos[:, :, None, :], sin[:, :, None, :])
    kv = (rmsnorm(ckv, kv_g) @ w_ukv).reshape(b, s, MLA_HEADS, MLA_NOPE + MLA_V)
    k_nope, v = kv[..., :MLA_NOPE], kv[..., MLA_NOPE:]
    k_rope = apply_rope(kr, cos, sin)
    nb = s // Q_BLOCK
    qn_b = q_nope.reshape(b, nb, Q_BLOCK, MLA_HEADS, MLA_NOPE).transpose(1, 0, 2, 3, 4)
    qr_b = q_rope.reshape(b, nb, Q_BLOCK, MLA_HEADS, MLA_ROPE).transpose(1, 0, 2, 3, 4)
    key_idx = jnp.arange(s)
    scale = MLA_QK ** -0.5

    def block(args):
        qn, qr, i = args
        sc = (jnp.einsum('bqhd,bkhd->bhqk', qn, k_nope)
              + jnp.einsum('bqhd,bkd->bhqk', qr, k_rope)).astype(jnp.float32) * scale
        q_idx = i * Q_BLOCK + jnp.arange(Q_BLOCK)
        mask = key_idx[None, :] <= q_idx[:, None]
        sc = jnp.where(mask, sc, -jnp.inf)
        p = jax.nn.softmax(sc, axis=-1).astype(v.dtype)
        return jnp.einsum('bhqk,bkhd->bqhd', p, v)

    o = lax.map(block, (qn_b, qr_b, jnp.arange(nb)))
    return o.transpose(1, 0, 2, 3, 4).reshape(b, s, MLA_HEADS * MLA_V)


def setup_inputs(seed: int = 0) -> dict:
    key = jax.random.key(seed)
    ks = jax.random.split(key, 24)
    nrm = jax.random.normal
    f32 = jnp.float32
    x = nrm(ks[0], (BATCH, SEQ, D_MODEL), f32)
    mem = nrm(ks[1], (BATCH, MEM_LEN, D_MODEL), f32)
    offs = jax.random.randint(ks[2], (BATCH, 1), 0, 1024, dtype=jnp.int32)
    positions = (offs + jnp.arange(SEQ, dtype=jnp.int32)[None, :]).astype(jnp.int32)
    gain = lambda k, shape: 1.0 + 0.02 * nrm(k, shape, f32)
    return {
        "x": x,
        "mem": mem,
        "positions": positions,
        "norm_g": gain(ks[3], (DEPTH, D_MODEL)),
        "mem_norm_g": gain(ks[4], (DEPTH, D_MODEL)),
        "w_mem_kv": nrm(ks[5], (DEPTH, D_MODEL, 2 * MEM_WIDTH), f32) * D_MODEL ** -0.5,
        "w_out": nrm(ks[6], (DEPTH, MIX_WIDTH, D_MODEL), f32) * MIX_WIDTH ** -0.5,
        "conv_w_in": nrm(ks[7], (N_CONV_LAYERS, D_MODEL, CONV_IN_COLS), f32) * D_MODEL ** -0.5,
        "conv_dw": nrm(ks[8], (N_CONV_LAYERS, CONV_KERNEL, CONV_WIDTH), f32) * CONV_KERNEL ** -0.5,
        "conv_dw_b": 0.02 * nrm(ks[9], (N_CONV_LAYERS, CONV_WIDTH), f32),
        "conv_ln_g": gain(ks[10], (N_CONV_LAYERS, CONV_WIDTH)),
        "conv_ln_b": 0.02 * nrm(ks[11], (N_CONV_LAYERS, CONV_WIDTH), f32),
        "mla_w_in": nrm(ks[12], (N_MLA_LAYERS, D_MODEL, MLA_IN_COLS), f32) * D_MODEL ** -0.5,
        "mla_q_norm_g": gain(ks[13], (N_MLA_LAYERS, Q_RANK)),
        "mla_w_uq": nrm(ks[14], (N_MLA_LAYERS, Q_RANK, MLA_HEADS * MLA_QK), f32) * Q_RANK ** -0.5,
        "mla_kv_norm_g": gain(ks[15], (N_MLA_LAYERS, KV_RANK)),
        "mla_w_ukv": nrm(ks[16], (N_MLA_LAYERS, KV_RANK, MLA_HEADS * (MLA_NOPE + MLA_V)), f32) * KV_RANK ** -0.5,
        "final_norm_g": gain(ks[17], (D_MODEL,)),
    }


def reference(x, mem, positions, norm_g, mem_norm_g, w_mem_kv, w_out, conv_w_in, conv_dw,
              conv_dw_b, conv_ln_g, conv_ln_b, mla_w_in, mla_q_norm_g, mla_w_uq,
              mla_kv_norm_g, mla_w_ukv, final_norm_g):
    cos, sin = rope_tables(positions)
    h = x
    for i in range(DEPTH):
        j = i // N_MIXERS
        u = rmsnorm(h, norm_g[i])
        if i % N_MIXERS == 0:
            proj = u @ conv_w_in[j]
            u_conv, q_mem, z = jnp.split(proj, [2 * CONV_WIDTH, 2 * CONV_WIDTH + MEM_WIDTH], axis=-1)
            y_main = conv_branch(u_conv, conv_dw[j], conv_dw_b[j], conv_ln_g[j], conv_ln_b[j])
        else:
            proj = u @ mla_w_in[j]
            c1 = Q_RANK
            c2 = c1 + KV_RANK
            c3 = c2 + MLA_ROPE
            c4 = c3 + MEM_WIDTH
            cq, ckv, kr, q_mem, z = jnp.split(proj, [c1, c2, c3, c4], axis=-1)
            y_main = mla_branch(cq, ckv, kr, cos, sin, mla_q_norm_g[j], mla_w_uq[j],
                                mla_kv_norm_g[j], mla_w_ukv[j])
        y_mem = mem_cross_attention(q_mem, mem, mem_norm_g[i], w_mem_kv[i])
        y = jnp.concatenate([y_main, y_mem], axis=-1) * jax.nn.silu(z)
        h = h + y @ w_out[i]
    return rmsnorm(h, final_norm_g)
```

```python
import contextlib
import numpy as np
import concourse.bass as bass
import concourse.mybir as mybir
from concourse.bass_utils import run_bass_kernel_spmd

F32 = mybir.dt.float32
BF16 = mybir.dt.bfloat16
I32 = mybir.dt.int32
AF = mybir.ActivationFunctionType
ALU = mybir.AluOpType

ENGS = ("pe", "act", "dve", "pool", "sp")
KR = 8
KD = 8

NCORES = 8
NB = 2
SEQ = 2048
D = 1024
T = 256
NTC = T // 128
NTILE = SEQ // T
MEM = 256
NV = 448
PI = float(np.pi)


class Buf:
    __slots__ = ("name", "t", "lw", "rd", "const")

    def __init__(self, name, t, const=False):
        self.name = name
        self.t = t
        self.lw = None
        self.rd = []
        self.const = const

    def __getitem__(self, k):
        return self.t[k]


class Op:
    __slots__ = ("eng", "fn", "idx", "dma", "sig", "waits", "clk", "didx", "signo")

    def __init__(self, eng, fn, dma):
        self.eng = eng
        self.fn = fn
        self.dma = dma
        self.sig = False
        self.waits = []
        self.clk = None
        self.didx = -1
        self.signo = -1


class Sched:
    def __init__(self, nc):
        self.nc = nc
        self.ops = {e: [] for e in ENGS}
        self.ndma = {e: 0 for e in ENGS}
        self.clock = {e: {f: -1 for f in ENGS} for e in ENGS}
        self.dma_seen = {e: set() for e in ENGS}
        self.dma_ops = {e: [] for e in ENGS}
        self.same_engine_sync = True

    def add(self, eng, fn, reads=(), writes=(), dma=False):
        op = Op(eng, fn, dma)
        deps = []
        wset = set(id(b) for b in writes)
        for b in reads:
            if b.lw is not None:
                deps.append(b.lw)
        for b in writes:
            if b.lw is not None:
                deps.append(b.lw)
            deps.extend(b.rd)
        for b in writes:
            b.lw = op
            b.rd = []
        for b in reads:
            if id(b) not in wset and not b.const:
                b.rd.append(op)
        self._resolve(op, deps)
        op.idx = len(self.ops[eng])
        self.ops[eng].append(op)
        if dma:
            op.didx = self.ndma[eng]
            self.ndma[eng] += 1
            self.dma_ops[eng].append(op)
        return op

    def _resolve(self, op, deps):
        eng = op.eng
        clk = self.clock[eng]
        best = {}
        for d in deps:
            if d.dma:
                if d not in self.dma_seen[eng]:
                    self.dma_seen[eng].add(d)
                    op.waits.append(d)
                continue
            f = d.eng
            if f == eng and not op.dma:
                if eng == "pe" or not self.same_engine_sync:
                    continue
            if clk[f] >= d.idx:
                continue
            if f not in best or best[f].idx < d.idx:
                best[f] = d
        for f, d in best.items():
            d.sig = True
            op.waits.append(d)
            for g, v in d.clk.items():
                if clk[g] < v:
                    clk[g] = v
            if clk[f] < d.idx:
                clk[f] = d.idx
        if not op.dma:
            op.clk = dict(clk)

    def barrier(self):
        lasts = []
        for e in ENGS:
            comp = [o for o in self.ops[e] if not o.dma and o.fn is not None]
            if comp:
                lasts.append(comp[-1])
            lasts.extend(self.dma_ops[e][-KD:])
        for e in ENGS:
            op = Op(e, None, False)
            self._resolve(op, list(lasts))
            op.idx = len(self.ops[e])
            self.ops[e].append(op)

    def emit(self):
        nc = self.nc
        with contextlib.ExitStack() as st:
            csem = {e: [st.enter_context(nc.semaphore(f"c_{e}_{i}")) for i in range(KR)] for e in ENGS}
            dsem = {e: [st.enter_context(nc.semaphore(f"d_{e}_{i}")) for i in range(KD)]
                    for e in ENGS if self.ndma[e] > 0}
            for e in ENGS:
                n = 0
                for o in self.ops[e]:
                    if o.sig and not o.dma:
                        o.signo = n
                        n += 1

            def wait_for(engobj, d):
                if d.dma:
                    engobj.wait_ge(dsem[d.eng][d.didx % KD], 16 * (d.didx // KD + 1))
                else:
                    engobj.wait_ge(csem[d.eng][d.signo % KR], d.signo // KR + 1)

            def run(e):
                def body(engobj):
                    for o in self.ops[e]:
                        for d in o.waits:
                            wait_for(engobj, d)
                        if o.fn is None:
                            continue
                        if o.dma:
                            m = o.didx
                            if m >= KD:
                                engobj.wait_ge(dsem[e][m % KD], 16 * (m // KD))
                            o.fn(engobj).then_inc(dsem[e][m % KD], 16)
                        else:
                            ins = o.fn(engobj)
                            if o.sig:
                                ins.then_inc(csem[e][o.signo % KR], 1)
                    nd = self.ndma[e]
                    for i in range(min(KD, nd)):
                        m = nd - 1 - i
                        engobj.wait_ge(dsem[e][m % KD], 16 * (m // KD + 1))
                return body

            with nc.Block() as block:
                block.tensor(run("pe"))
                block.scalar(run("act"))
                block.vector(run("dve"))
                block.gpsimd(run("pool"))
                block.sync(run("sp"))


class Ring:
    def __init__(self, items):
        self.items = items
        self.i = 0

    def next(self):
        it = self.items[self.i % len(self.items)]
        self.i += 1
        return it


def weight_groups(layer):
    g = {}
    wm = ("wmem", layer)
    for i in range(2):
        g[f"m{layer}k{i}"] = (wm, 0, 1024, i * 256, 256)
        g[f"m{layer}v{i}"] = (wm, 0, 1024, 512 + i * 256, 256)
    if layer == 0:
        for c in range(12):
            g[f"ag{c}"] = (("w0", None), 0, 1024, c * 256, 256)
        for i in range(8):
            g[f"z0_{i}"] = (("w0", None), 0, 1024, 3072 + i * 256, 256)
        for i in range(2):
            g[f"qm0_{i}"] = (("w0", None), 0, 1024, 5120 + i * 256, 256)
    else:
        for i in range(2):
            g[f"cq{i}"] = (("w1", None), 0, 1024, i * 256, 256)
        g["ckv"] = (("w1", None), 0, 1024, 512, 256)
        g["kr"] = (("w1", None), 0, 1024, 768, 128)
        for i in range(2):
            g[f"qm1_{i}"] = (("w1", None), 0, 1024, 896 + i * 256, 256)
        for i in range(8):
            g[f"z1_{i}"] = (("w1", None), 0, 1024, 1408 + i * 256, 256)
        for i in range(6):
            g[f"uq{i}"] = (("wuq", None), 0, 512, i * 512, 512)
        for i in range(2):
            g[f"uk{i}"] = (("wuk", None), 0, 256, i * 768, 768)
            g[f"uv{i}"] = (("wuv", None), 0, 256, i * 768, 768)
    for i in range(8):
        g[f"o{layer}_{i}"] = (("wout", layer), i * 256, 256, 0, 1024)
    return g


def tile_order(layer):
    if layer == 0:
        o = ["ag0"]
        for c in range(12):
            if c < 11:
                o.append(f"ag{c + 1}")
            if c < 8:
                o.append(f"z0_{c}")
            if c == 9:
                o += [f"qm0_{i}" for i in range(2)]
        return o + [f"o0_{i}" for i in (6, 7, 0, 1, 2, 3, 4, 5)]
    else:
        o = ["cq0", "cq1", "ckv", "kr"] + [f"qm1_{i}" for i in range(2)] + ["uk0", "uk1", "uv0", "uv1"]
        o += ["z1_6", "z1_7", "uq0"]
        for h in range(12):
            if h % 2 == 0 and h // 2 < 6:
                o.append(f"z1_{h // 2}")
            if h + 1 < 12 and (h + 1) % 2 == 0:
                o.append(f"uq{(h + 1) // 2}")
    return o + [f"o{layer}_{i}" for i in (6, 7, 0, 1, 2, 3, 4, 5)]


def mem_order(layer):
    return [f"m{layer}k0", f"m{layer}k1", f"m{layer}v0", f"m{layer}v1"]


def build_program(layers=(0, 1), nseq=NB, ntile=NTILE, dumps=None):
    nc = bass.Bass("TRN2", target_bir_lowering=False)
    S = Sched(nc)
    RS = 5

    def dram(name, shape, dt, kind):
        return nc.dram_tensor(name, list(shape), dt, kind=kind)

    x_d = dram("x", [NB, SEQ, D], F32, "ExternalInput")
    mem_d = dram("mem", [NB, MEM, D], F32, "ExternalInput")
    pos_d = dram("pos", [NB, SEQ], I32, "ExternalInput")
    vecs_d = dram("vecs", [128, NV], F32, "ExternalInput")
    fg_d = dram("fg", [1, D], F32, "ExternalInput")
    cst_d = dram("cst", [128, 128 + 2 * T], F32, "ExternalInput")
    wsrc = {}
    if 0 in layers:
        wsrc["w0"] = dram("w0", [1024, 5632], F32, "ExternalInput")
    if 1 in layers:
        wsrc["w1"] = dram("w1", [1024, 3456], F32, "ExternalInput")
        wsrc["wuq"] = dram("wuq", [512, 3072], F32, "ExternalInput")
        wsrc["wuk"] = dram("wuk", [256, 1536], F32, "ExternalInput")
        wsrc["wuv"] = dram("wuv", [256, 1536], F32, "ExternalInput")
    wsrc["wout"] = dram("wout", [2, 2048, 1024], F32, "ExternalInput")
    wsrc["wmem"] = dram("wmem", [2, 1024, 1024], F32, "ExternalInput")
    out_d = dram("out", [NB, SEQ, D], F32, "ExternalOutput")
    h1_d = nc.dram_tensor("h1s", [NB, SEQ, D], F32) if len(layers) == 2 else None

    def sb(name, shape, dt, const=False):
        return Buf(name, nc.alloc_sbuf_tensor("s_" + name, list(shape), dt), const)

    def A(eng, fn, r=(), w=(), dma=False):
        return S.add(eng, fn, reads=r, writes=w, dma=dma)

    vecs = sb("vecs", [128, NV], F32, const=True)
    ident = sb("ident", [128, 128], BF16, const=True)
    ones = sb("ones", [128, 128], BF16, const=True)
    masks = sb("masks", [128, 2, T], BF16, const=True)
    epsc = sb("epsc", [128, 2], F32, const=True)
    fgt = sb("fgt", [128, D], F32, const=True)
    A("sp", lambda e: e.dma_start(out=vecs[:, :], in_=vecs_d[:, :]), w=[vecs], dma=True)
    A("sp", lambda e: e.dma_start(out=fgt[:, :], in_=fg_d[0:1, :].partition_broadcast(128)), w=[fgt], dma=True)
    A("pool", lambda e: e.memset(ones[:, :], 1.0), w=[ones])
    A("pool", lambda e: e.memset(epsc[:, 0:1], 1e-6), w=[epsc])
    A("pool", lambda e: e.memset(epsc[:, 1:2], 1e-5), w=[epsc])
    EPS6 = epsc[:, 0:1]
    EPS5 = epsc[:, 1:2]
    V_NG = {0: 0, 1: 8}
    V_MG = {0: 16, 1: 24}
    V_DWB, V_LNG, V_LNB, V_QG, V_KVG, V_INVF, V_SGN, V_DW = 32, 44, 56, 68, 72, 74, 75, 76

    P = [Buf(f"P{i}", nc.alloc_psum_tensor(f"ps_P{i}", [128, 512], F32)) for i in range(8)]
    pT = P[7]
    pTv = P[7].t[:, :].bitcast(BF16)
    pp_l = {0: Ring(P[0:4]), 1: Ring(P[0:3])}
    cur = {"pp": pp_l[layers[0]], "layer": layers[0]}
    convp = Ring(P[4:6])
    Sbank = Ring([P[3], P[4], P[7]])
    podp = Ring(P[5:7])

    ring = [sb(f"wr{i}", [128, 2048], BF16) for i in range(RS)]
    KmTs = [sb(f"KmT{b}", [128, 4, MEM], BF16) for b in range(NB)]
    Vms = [sb(f"Vm{b}", [128, 2, 512], BF16) for b in range(NB)]
    xpool = Ring([sb(f"xc{i}", [128, D], F32) for i in range(4)])
    cstf = xpool.next()
    A("sp", lambda e: e.dma_start(out=cstf[:, 0:128 + 2 * T], in_=cst_d[:, :]), w=[cstf], dma=True)
    A("dve", lambda e: e.tensor_copy(out=ident[:, :], in_=cstf[:, 0:128]), r=[cstf], w=[ident])
    A("dve", lambda e: e.tensor_copy(out=masks.t.rearrange("p a b -> p (a b)"), in_=cstf[:, 128:128 + 2 * T]),
      r=[cstf], w=[masks])
    ubp = Ring([sb(f"ub{i}", [128, D], BF16) for i in range(2)])
    ss = sb("ss", [128, 4], F32)
    rstd = sb("rstd", [128, 4], F32)
    ss2 = sb("ss2", [128, 4], F32)
    rstd2 = sb("rstd2", [128, 4], F32)
    uTs = Ring([sb(f"uT{i}", [128, 8, T], BF16) for i in range(2)])
    sz_t = nc.alloc_sbuf_tensor("s_sz", [128, 16, T], BF16)
    sz = [Buf(f"sz{i}", sz_t[:, i, :]) for i in range(16)]
    qm_t = nc.alloc_sbuf_tensor("s_qmT", [128, 4, T], BF16)
    qmT = [Buf(f"qm{i}", qm_t[:, i, :]) for i in range(4)]
    etp = Ring([sb(f"et{i}", [128, 2 * T], BF16) for i in range(4)])
    ftp = Ring([sb(f"ft{i}", [128, T], F32) for i in range(6)])
    btp = Ring([sb(f"bt{i}", [128, T], BF16) for i in range(4)])

    ARENA = 51200
    arena = nc.alloc_sbuf_tensor("s_arena", [128, ARENA], BF16)
    KTb = [[Buf(f"KT{h}_{j}", arena[:, h * SEQ + j * T: h * SEQ + (j + 1) * T]) for j in range(NTILE)]
           for h in range(12)]
    voff = 12 * SEQ
    Vb = [Buf(f"V{j}", arena[:, voff + j * NTC * 1536: voff + (j + 1) * NTC * 1536]
              .rearrange("p (c n) -> p c n", c=NTC)) for j in range(NTILE)]
    koff = voff + 16 * 1536
    kro = [Buf(f"kro{j}", arena[:, koff + j * T: koff + (j + 1) * T]) for j in range(NTILE)]
    kro_all = Buf("kro_all", arena[:, koff:koff + SEQ])
    assert koff + SEQ <= ARENA
    NSTG = 2
    stg32 = [Buf(f"stg32_{i}", arena[:, i * 6144: i * 6144 + 4096].bitcast(F32)) for i in range(NSTG)]
    stg16 = [Buf(f"stg16_{i}", arena[:, i * 6144 + 4096: i * 6144 + 6144]) for i in range(NSTG)]
    o0 = NSTG * 6144
    GW = 30 + T
    glu = [Buf(f"glu{c}", arena[:, o0 + c * GW: o0 + (c + 1) * GW]) for c in range(12)]
    o1 = o0 + 12 * GW
    cvb = [Buf(f"cvb{c}", arena[:, o1 + c * T: o1 + (c + 1) * T]) for c in range(12)]
    o2 = o1 + 12 * T
    NDG = 4
    KLO = 17
    diag_lo = [Buf(f"diaglo{i}", arena[:, o2 + i * 31 * 128: o2 + i * 31 * 128 + KLO * 128]
                   .rearrange("p (k c) -> p k c", k=KLO)) for i in range(NDG)]
    diag_hi = [Buf(f"diaghi{i}", arena[:, o2 + i * 31 * 128 + KLO * 128: o2 + (i + 1) * 31 * 128]
                   .rearrange("p (k c) -> p k c", k=31 - KLO)) for i in range(NDG)]
    o3 = o2 + NDG * 31 * 128
    lnt = [Buf(f"lnt{i}", arena[:, o3 + i * 2 * T: o3 + (i + 1) * 2 * T].bitcast(F32)) for i in range(4)]
    assert o3 + 8 * T <= ARENA

    if 1 in layers:
        cq_t = nc.alloc_sbuf_tensor("s_cq", [128, 4, T], F32)
        cq = [Buf(f"cq{i}", cq_t[:, i, :]) for i in range(4)]
        cqnT = sb("cqnT", [128, 4, T], BF16)
        ckv_t = nc.alloc_sbuf_tensor("s_ckv", [128, 2, T], F32)
        ckv = [Buf(f"ckv{i}", ckv_t[:, i, :]) for i in range(2)]
        ckvnT = sb("ckvnT", [128, 2, T], BF16)
        posi = sb("posi", [64, T], I32)
        rp = [sb(f"rp{i}", [64, T], F32) for i in range(5)]
        kint = sb("kint", [64, T], I32)
        cs_ring = Ring([(sb(f"cosT{i}", [64, T], F32), sb(f"sinT{i}", [64, T], F32)) for i in range(2)])
        ropecur = {}
        qnp = Ring([sb(f"qn{i}", [128, T], BF16) for i in range(2)])
        qr_bufs = [sb(f"qr{i}", [128, T], BF16) for i in range(2)]
        qrp = Ring(qr_bufs)

    groups = {}
    for l in layers:
        groups.update(weight_groups(l))
    scr = {gid: Buf(f"ws_{gid}", nc.dram_tensor(f"ws_{gid}", [128, 2048], BF16)) for gid in groups}
    prep_state = {"n": 0}
    cast_engs = ("dve", "act")

    def gshape(gid):
        _, r0, nr, c0, ncol = groups[gid]
        return nr // 128, ncol

    def prep(gid):
        (sname, sidx), r0, nr, c0, ncol = groups[gid]
        kcn = nr // 128
        cnt = kcn * ncol
        src = wsrc[sname]
        if sidx is None:
            sap = src[r0:r0 + nr, c0:c0 + ncol]
        else:
            sap = src[sidx, r0:r0 + nr, c0:c0 + ncol]
        sap = sap.rearrange("(kc p) c -> p kc c", p=128)
        i = prep_state["n"]
        prep_state["n"] += 1
        s32, s16 = stg32[i % NSTG], stg16[i % NSTG]
        A("pool", lambda e: e.dma_start(out=s32[:, 0:cnt].rearrange("p (k c) -> p k c", k=kcn), in_=sap),
          w=[s32], dma=True)
        A("act", lambda e: e.activation(out=s16[:, 0:cnt], in_=s32[:, 0:cnt], func=AF.Copy), r=[s32], w=[s16])
        A("act", lambda e: e.dma_start(out=scr[gid][:, 0:cnt], in_=s16[:, 0:cnt]), r=[s16], w=[scr[gid]], dma=True)

    class WStream:
        def __init__(self, order):
            self.order = order
            self.i = 0
            self.issued = 0

        def _issue(self, n):
            gid = self.order[n]
            slot = ring[n % RS]
            kcn, ncol = gshape(gid)
            cnt = kcn * ncol
            self.need(gid)
            assert scr[gid].lw is not None, gid
            A("sp", lambda e: e.dma_start(out=slot[:, 0:cnt], in_=scr[gid][:, 0:cnt]),
              r=[scr[gid]], w=[slot], dma=True)

        def get(self, gid):
            assert self.order[self.i] == gid, (self.order[self.i], gid)
            while self.issued < min(len(self.order), self.i + RS):
                self._issue(self.issued)
                self.issued += 1
            slot = ring[self.i % RS]
            self.i += 1
            kcn, ncol = gshape(gid)
            return slot, slot[:, 0:kcn * ncol].rearrange("p (k c) -> p k c", k=kcn)

    order = []
    for l in layers:
        for b in range(nseq):
            order += mem_order(l)
        for b in range(nseq):
            for j in range(ntile):
                order += tile_order(l)
    WS = WStream(order)

    MMF = {
        "proj": lambda e, o, l, r, st, sp: e.matmul(o, lhsT=l, rhs=r, start=st, stop=sp),
        "conv": lambda e, o, l, r, st, sp: e.matmul(o, lhsT=l, rhs=r, start=st, stop=sp),
        "stat": lambda e, o, l, r, st, sp: e.matmul(o, lhsT=l, rhs=r, start=st, stop=sp, skip_group_check=True),
        "S": lambda e, o, l, r, st, sp: e.matmul(o, lhsT=l, rhs=r, start=st, stop=sp, skip_group_check=True),
        "PV": lambda e, o, l, r, st, sp: e.matmul(o, lhsT=l, rhs=r, start=st, stop=sp, skip_group_check=True),
        "den": lambda e, o, l, r, st, sp: e.matmul(o, lhsT=l, rhs=r, start=st, stop=sp, skip_group_check=True),
        "out": lambda e, o, l, r, st, sp: e.matmul(o, lhsT=l, rhs=r, start=st, stop=sp),
        "mem": lambda e, o, l, r, st, sp: e.matmul(o, lhsT=l, rhs=r, start=st, stop=sp, skip_group_check=True),
        "kv": lambda e, o, l, r, st, sp: e.matmul(o, lhsT=l, rhs=r, start=st, stop=sp),
        "q": lambda e, o, l, r, st, sp: e.matmul(o, lhsT=l, rhs=r, start=st, stop=sp),
    }

    def mm(pbuf, out_ap, lhsT, rhs, start, stop, reads, tag="proj"):
        f = MMF[tag]
        A("pe", lambda e: f(e, out_ap, lhsT, rhs, start, stop), r=reads, w=[pbuf])

    def rsqrt_small(n, ssb, rsb):
        A("act", lambda e: e.activation(out=rsb[:, 0:n], in_=ssb[:, 0:n], func=AF.Ln, scale=1.0 / D, bias=EPS6),
          r=[ssb, epsc], w=[rsb])
        A("act", lambda e: e.activation(out=rsb[:, 0:n], in_=rsb[:, 0:n], func=AF.Exp, scale=-0.5),
          r=[rsb], w=[rsb])

    def norm_A_steps(xcs):
        n = len(xcs)
        ubs = [ubp.next() for _ in range(n)]
        steps = []
        for tc, xc in enumerate(xcs):
            ub = ubs[tc]
            steps.append(lambda xc=xc, ub=ub, tc=tc: A(
                "act", lambda e: e.activation(out=ub[:, :], in_=xc[:, :], func=AF.Square,
                                              accum_out=ss[:, tc:tc + 1]), r=[xc], w=[ub, ss]))
        steps.append(lambda: rsqrt_small(n, ss, rstd))
        for tc, xc in enumerate(xcs):
            ub = ubs[tc]
            steps.append(lambda xc=xc, ub=ub, tc=tc: A(
                "act", lambda e: e.activation(out=ub[:, :], in_=xc[:, :], func=AF.Copy,
                                              scale=rstd[:, tc:tc + 1]), r=[xc, rstd], w=[ub]))
        return ubs, steps

    def norm_A(xcs):
        ubs, steps = norm_A_steps(xcs)
        for st in steps:
            st()
        return ubs

    def tr_banks():
        if cur["layer"] == 0:
            return [P[7], P[2]]
        return [cur["pp"].next(), cur["pp"].next()]

    def norm_B(ubs, gcol0, dstT):
        gap = vecs[:, gcol0:gcol0 + 8].unsqueeze(2).to_broadcast([128, 8, 128])
        banks = tr_banks()
        for tc, ub in enumerate(ubs):
            bank = banks[tc % len(banks)]
            bv = bank.t[:, :].bitcast(BF16)
            for kc in range(8):
                A("pe", lambda e, kc=kc, ub=ub, bv=bv: e.transpose(out=bv[:, kc * 128:(kc + 1) * 128],
                                                                   in_=ub[:, kc * 128:(kc + 1) * 128],
                                                                   identity=ident[:, :]),
                  r=[ub, ident], w=[bank])
            A("dve", lambda e, tc=tc, bv=bv: e.tensor_tensor(out=dstT[:, :, tc * 128:(tc + 1) * 128],
                                                             in0=bv.rearrange("p (k c) -> p k c", k=8), in1=gap,
                                                             op=ALU.mult), r=[bank, vecs], w=[dstT])

    def norm_T(xcs, gcol0, dstT):
        norm_B(norm_A(xcs), gcol0, dstT)

    def fm_proj(pb, m, slot, sv, c0, src, kcn, tag="proj"):
        for kc in range(kcn):
            mm(pb, pb[0:m, 0:T], sv[:, kc, c0:c0 + m], src[:, kc, :], kc == 0, kc == kcn - 1, [slot, src], tag=tag)

    def mem_prep(l, b):
        pp = cur["pp"]
        KmT, Vm = KmTs[b], Vms[b]
        uT = uTs.next()
        xcs = [xpool.next() for _ in range(2)]
        for mc, xc in enumerate(xcs):
            A("sp", lambda e, xc=xc, mc=mc: e.dma_start(out=xc[:, :], in_=mem_d[b, mc * 128:(mc + 1) * 128, :]),
              w=[xc], dma=True)
        norm_T(xcs, V_MG[l], uT)
        for i in range(2):
            slot, sv = WS.get(f"m{l}k{i}")
            for hh in range(2):
                h = 2 * i + hh
                pb = pp.next()
                fm_proj(pb, 128, slot, sv, hh * 128, uT, 8)
                A("act", lambda e, pb=pb, h=h: e.activation(out=KmT[:, h, :], in_=pb[:, 0:MEM], func=AF.Copy),
                  r=[pb], w=[KmT])
        for i in range(2):
            slot, sv = WS.get(f"m{l}v{i}")
            for mc in range(2):
                pb = pp.next()
                for kc in range(8):
                    mm(pb, pb[:, 0:256], uT[:, kc, mc * 128:(mc + 1) * 128], sv[:, kc, :], kc == 0, kc == 7,
                       [slot, uT])
                A("act", lambda e, pb=pb, mc=mc, i=i: e.activation(out=Vm[:, mc, i * 256:(i + 1) * 256],
                                                                  in_=pb[:, 0:256], func=AF.Copy),
                  r=[pb], w=[Vm])

    def zchunk(pz, dst):
        th = btp.next()
        A("act", lambda e: e.activation(out=th[:, :], in_=pz[:, 0:T], func=AF.Tanh, scale=0.5), r=[pz], w=[th])
        A("dve", lambda e: e.scalar_tensor_tensor(out=dst[:, :], in0=th[:, :], scalar=1.0, in1=pz[:, 0:T],
                                                  op0=ALU.add, op1=ALU.mult), r=[th, pz], w=[dst])

    def softmax_finish(pod, dst):
        rden = ftp.next()
        A("dve", lambda e: e.reciprocal(out=rden[:, :], in_=pod[:, T:2 * T]), r=[pod], w=[rden])
        tt = ftp.next()
        A("dve", lambda e: e.scalar_tensor_tensor(out=tt[:, :], in0=pod[:, 0:T], scalar=0.5, in1=rden[:, :],
                                                  op0=ALU.mult, op1=ALU.mult), r=[pod, rden], w=[tt])
        A("pool", lambda e: e.tensor_tensor(out=dst[:, :], in0=dst[:, :], in1=tt[:, :], op=ALU.mult),
          r=[dst, tt], w=[dst])

    def mem_attn_all(b, srings, prings):
        sc = float(128 ** -0.5)
        KmT, Vm = KmTs[b], Vms[b]

        def scores(h):
            pS = srings.next()
            for mc in range(2):
                mm(pS, pS[:, mc * T:(mc + 1) * T], KmT[:, h, mc * 128:(mc + 1) * 128], qmT[h][:, :], mc == 0, True,
                   [KmT, qmT[h]], tag="mem")
            return pS

        q = [scores(0), scores(1)]
        for h in range(4):
            pS = q.pop(0)
            if h + 2 < 4:
                q.append(scores(h + 2))
            et = etp.next()
            pod = prings.next()
            A("act", lambda e, pS=pS, et=et: e.activation(out=et[:, :], in_=pS[:, 0:2 * T], func=AF.Exp, scale=sc),
              r=[pS], w=[et])
            for mc in range(2):
                mm(pod, pod[:, 0:T], Vm[:, mc, h * 128:(h + 1) * 128], et[:, mc * T:(mc + 1) * T], mc == 0, mc == 1,
                   [Vm, et], tag="PV")
            for mc in range(2):
                mm(pod, pod[:, T:2 * T], ones[:, :], et[:, mc * T:(mc + 1) * T], False, mc == 1, [ones, et],
                   tag="den")
            softmax_finish(pod, sz[12 + h])

    dramB = {}

    def dbuf(d, b, j):
        key = (d.name, b, j)
        if key not in dramB:
            dramB[key] = Buf("B_%s_%d_%d" % key, d)
        return dramB[key]

    def load_tile(src_d, b, j):
        xcs = [xpool.next() for _ in range(NTC)]
        for tc, xc in enumerate(xcs):
            r0 = j * T + tc * 128
            A("sp", lambda e, xc=xc, r0=r0: e.dma_start(out=xc[:, :], in_=src_d[b, r0:r0 + 128, :]),
              r=[dbuf(src_d, b, j)], w=[xc], dma=True)
        return xcs

    def tile_pre_a(l, src_d, b, j):
        xcs = load_tile(src_d, b, j)
        return xcs, norm_A(xcs)

    def tile_pre_steps(l, src_d, b, j):
        xcs = load_tile(src_d, b, j)
        ubs, steps = norm_A_steps(xcs)
        return (xcs, ubs), steps

    def tile_pre_b(l, st):
        xcs, ubs = st
        uT = uTs.next()
        norm_B(ubs, V_NG[l], uT)
        return xcs, uT

    def out_proj(l, xcs, dst_d, b, j, final, gorder=(0, 1, 2, 3, 4, 5, 6, 7)):
        acc = [P[0], P[1], P[2], P[3]]
        for gi_, i in enumerate(gorder):
            slot, sv = WS.get(f"o{l}_{i}")
            for kk in range(2):
                kc = 2 * i + kk
                first = (gi_ == 0 and kk == 0)
                last = (gi_ == 7 and kk == 1)
                for tc in range(NTC):
                    for half in range(2):
                        pb = acc[tc * 2 + half]
                        mm(pb, pb[:, 0:512], sz[kc][:, tc * 128:(tc + 1) * 128],
                           sv[:, kk, half * 512:(half + 1) * 512], first, last, [slot, sz[kc]], tag="out")
        for tc, xc in enumerate(xcs):
            for half in range(2):
                pb = acc[tc * 2 + half]
                A("dve", lambda e, xc=xc, pb=pb, half=half: e.tensor_tensor(
                    out=xc[:, half * 512:(half + 1) * 512], in0=pb[:, 0:512],
                    in1=xc[:, half * 512:(half + 1) * 512], op=ALU.add), r=[pb, xc], w=[xc])
        if final:
            for tc, xc in enumerate(xcs):
                ub = ubp.next()
                A("act", lambda e, xc=xc, ub=ub, tc=tc: e.activation(out=ub[:, :], in_=xc[:, :], func=AF.Square,
                                                                    accum_out=ss2[:, tc:tc + 1]),
                  r=[xc], w=[ub, ss2])
            rsqrt_small(NTC, ss2, rstd2)
            for tc, xc in enumerate(xcs):
                A("dve", lambda e, xc=xc, tc=tc: e.scalar_tensor_tensor(
                    out=xc[:, :], in0=xc[:, :], scalar=rstd2[:, tc:tc + 1], in1=fgt[:, :],
                    op0=ALU.mult, op1=ALU.mult), r=[xc, rstd2, fgt], w=[xc])
        dstB = dbuf(dst_d, b, j)
        for tc, xc in enumerate(xcs):
            r0 = j * T + tc * 128
            A("pool", lambda e, xc=xc, r0=r0: e.dma_start(out=dst_d[b, r0:r0 + 128, :], in_=xc[:, :]),
              r=[xc], w=[dstB], dma=True)

    dstate = {"built": 0, "total": 0}

    def ensure_diag(g):
        while dstate["built"] <= min(g, dstate["total"] - 1):
            gi = dstate["built"]
            c = gi % 12
            dl, dh = diag_lo[gi % NDG], diag_hi[gi % NDG]
            w0_ = V_DW + c * 31
            A("pool", lambda e, dl=dl, w0_=w0_: e.tensor_tensor(
                out=dl[:, :, :], in0=ident[:, :].unsqueeze(1).to_broadcast([128, KLO, 128]),
                in1=vecs[:, w0_:w0_ + KLO].unsqueeze(2).to_broadcast([128, KLO, 128]),
                op=ALU.mult), r=[ident, vecs], w=[dl])
            A("dve", lambda e, dh=dh, w0_=w0_: e.tensor_tensor(
                out=dh[:, :, :], in0=ident[:, :].unsqueeze(1).to_broadcast([128, 31 - KLO, 128]),
                in1=vecs[:, w0_ + KLO:w0_ + 31].unsqueeze(2).to_broadcast([128, 31 - KLO, 128]),
                op=ALU.mult), r=[ident, vecs], w=[dh])
            dstate["built"] += 1

    def l0_tile(b, j, tix, pre, nxt_a, nxt_b, dst_d):
        pp = cur["pp"]
        xcs, uT = pre
        nst = None
        nsteps = None
        nxt = None
        s12 = P[6]

        def ag(c):
            slot, sv = WS.get(f"ag{c}")
            pa, pg = P[0], P[1]
            fm_proj(pa, 128, slot, sv, 0, uT, 8)
            fm_proj(pg, 128, slot, sv, 128, uT, 8)
            th = ftp.next()
            A("act", lambda e: e.activation(out=th[:, :], in_=pg[:, 0:T], func=AF.Tanh, scale=0.5), r=[pg], w=[th])
            g = glu[c]
            if j == 0:
                A("dve", lambda e: e.memset(g[:, 0:30], 0.0), w=[g])
            A("dve", lambda e: e.scalar_tensor_tensor(out=g[:, 30:30 + T], in0=th[:, :], scalar=1.0,
                                                      in1=pa[:, 0:T], op0=ALU.add, op1=ALU.mult),
              r=[th, pa, g], w=[g])

        def zgroup(i):
            slot, sv = WS.get(f"z0_{i}")
            for hh in range(2):
                pz = P[2 + hh]
                fm_proj(pz, 128, slot, sv, hh * 128, uT, 8)
                zchunk(pz, sz[2 * i + hh])

        def stats(c, sq):
            mm(s12, s12[:, 0:T], ones[:, :], cvb[c][:, :], c == 0, c == 11, [ones, cvb[c]], tag="stat")
            mm(s12, s12[:, T:2 * T], ones[:, :], sq[:, :], False, c == 11, [ones, sq], tag="stat")

        ensure_diag(tix * 12 + 2)
        ag(0)
        prev_sq = None
        for c in range(12):
            if c < 11:
                ag(c + 1)
            g = glu[c]
            gi = tix * 12 + c
            dl, dh = diag_lo[gi % NDG], diag_hi[gi % NDG]
            pc = convp.next()
            for k in range(31):
                if k < KLO:
                    mm(pc, pc[:, 0:T], dl[:, k, :], g[:, k:k + T], k == 0, k == 30, [dl, g], tag="conv")
                else:
                    mm(pc, pc[:, 0:T], dh[:, k - KLO, :], g[:, k:k + T], k == 0, k == 30, [dh, g], tag="conv")
            ensure_diag(gi + 3)
            cv = cvb[c]
            A("act", lambda e, pc=pc, cv=cv, c=c: e.activation(out=cv[:, :], in_=pc[:, 0:T], func=AF.Identity,
                                                              scale=0.5, bias=vecs[:, V_DWB + c:V_DWB + c + 1]),
              r=[pc, vecs], w=[cv])
            sq = btp.next()
            A("act", lambda e, pc=pc, sq=sq, c=c: e.activation(out=sq[:, :], in_=pc[:, 0:T], func=AF.Square,
                                                              scale=0.5, bias=vecs[:, V_DWB + c:V_DWB + c + 1]),
              r=[pc, vecs], w=[sq])
            if prev_sq is not None:
                stats(c - 1, prev_sq)
            prev_sq = sq
            A("dve", lambda e, g=g: e.tensor_copy(out=g[:, 0:30], in_=g[:, T:T + 30]), r=[g], w=[g])
            if c < 8:
                zgroup(c)
            if c == 0 and nxt_a is not None:
                nst, nsteps = nxt_a()
            if nst is not None and nsteps:
                nsteps.pop(0)()
            if c == 8 and nst is not None:
                while nsteps:
                    nsteps.pop(0)()
                nxt = nxt_b(nst)
            if c % 2 == 1:
                bg_prep()
            if c == 9:
                qring = Ring([P[2], P[3], P[7]])
                for i in range(2):
                    slot, sv = WS.get(f"qm0_{i}")
                    for hh in range(2):
                        h = 2 * i + hh
                        pq = qring.next()
                        fm_proj(pq, 128, slot, sv, hh * 128, uT, 8)
                        A("act", lambda e, pq=pq, h=h: e.activation(out=qmT[h][:, :], in_=pq[:, 0:T], func=AF.Copy),
                          r=[pq], w=[qmT[h]])
                mem_attn_all(b, Ring([P[2], P[3], P[7]]), Ring([P[0], P[1]]))
        stats(11, prev_sq)
        mean, msq, rs, nmr = lnt
        A("dve", lambda e: e.tensor_scalar(out=mean[:, :], in0=s12[:, 0:T], scalar1=1.0 / 1536, scalar2=None,
                                           op0=ALU.mult), r=[s12], w=[mean])
        A("dve", lambda e: e.tensor_tensor(out=msq[:, :], in0=mean[:, :], in1=mean[:, :], op=ALU.mult),
          r=[mean], w=[msq])
        A("dve", lambda e: e.scalar_tensor_tensor(out=rs[:, :], in0=s12[:, T:2 * T], scalar=1.0 / 1536,
                                                  in1=msq[:, :], op0=ALU.mult, op1=ALU.subtract),
          r=[s12, msq], w=[rs])
        A("act", lambda e: e.activation(out=rs[:, :], in_=rs[:, :], func=AF.Ln, bias=EPS5), r=[rs, epsc], w=[rs])
        A("act", lambda e: e.activation(out=rs[:, :], in_=rs[:, :], func=AF.Exp, scale=-0.5), r=[rs], w=[rs])
        A("dve", lambda e: e.scalar_tensor_tensor(out=nmr[:, :], in0=mean[:, :], scalar=-1.0, in1=rs[:, :],
                                                  op0=ALU.mult, op1=ALU.mult), r=[mean, rs], w=[nmr])
        def ln_s12(c):
            t1 = ftp.next()
            cv = cvb[c]
            A("dve", lambda e: e.tensor_tensor(out=t1[:, :], in0=cv[:, :], in1=rs[:, :], op=ALU.mult),
              r=[cv, rs], w=[t1])
            A("dve", lambda e: e.tensor_tensor(out=t1[:, :], in0=t1[:, :], in1=nmr[:, :], op=ALU.add),
              r=[t1, nmr], w=[t1])
            sl = btp.next()
            A("act", lambda e: e.activation(out=sl[:, :], in_=t1[:, :], func=AF.Silu,
                                            scale=vecs[:, V_LNG + c:V_LNG + c + 1],
                                            bias=vecs[:, V_LNB + c:V_LNB + c + 1]), r=[t1, vecs], w=[sl])
            return sl

        def ln_s3(c, sl):
            dst = sz[c]
            A("dve", lambda e: e.scalar_tensor_tensor(out=dst[:, :], in0=sl[:, :], scalar=0.5, in1=dst[:, :],
                                                      op0=ALU.mult, op1=ALU.mult), r=[dst, sl], w=[dst])

        sl_prev = ln_s12(0)
        for c in range(1, 12):
            sl_c = ln_s12(c)
            ln_s3(c - 1, sl_prev)
            sl_prev = sl_c
        ln_s3(11, sl_prev)
        out_proj(0, xcs, dst_d, b, j, final=False, gorder=(6, 7, 0, 1, 2, 3, 4, 5))
        return nxt

    def rope_tables_steps(b, j):
        t0 = j * T
        cosT, sinT = cs_ring.next()
        posf, kf, r0, msk, ang = rp
        C1 = 6.28125
        C2 = 2 * PI - C1

        def ph1():
            A("sp", lambda e: e.dma_start(out=posi[:, :], in_=pos_d[b:b + 1, t0:t0 + T].partition_broadcast(64)),
              w=[posi], dma=True)
            A("dve", lambda e: e.tensor_copy(out=posf[:, :], in_=posi[:, :]), r=[posi], w=[posf])
            A("dve", lambda e: e.tensor_scalar(out=posf[:, :], in0=posf[:, :],
                                               scalar1=vecs[0:64, V_INVF:V_INVF + 1], scalar2=None, op0=ALU.mult),
              r=[posf, vecs], w=[posf])
            A("dve", lambda e: e.tensor_scalar(out=kf[:, :], in0=posf[:, :], scalar1=1.0 / (2 * PI), scalar2=None,
                                               op0=ALU.mult), r=[posf], w=[kf])
            A("dve", lambda e: e.tensor_copy(out=kint[:, :], in_=kf[:, :]), r=[kf], w=[kint])
            A("dve", lambda e: e.tensor_copy(out=kf[:, :], in_=kint[:, :]), r=[kint], w=[kf])
            A("dve", lambda e: e.scalar_tensor_tensor(out=r0[:, :], in0=kf[:, :], scalar=-C1, in1=posf[:, :],
                                                      op0=ALU.mult, op1=ALU.add), r=[kf, posf], w=[r0])
            A("dve", lambda e: e.scalar_tensor_tensor(out=r0[:, :], in0=kf[:, :], scalar=-C2, in1=r0[:, :],
                                                      op0=ALU.mult, op1=ALU.add), r=[kf, r0], w=[r0])

        def ph2():
            A("dve", lambda e: e.tensor_scalar(out=ang[:, :], in0=r0[:, :], scalar1=0.5 * PI, scalar2=None,
                                               op0=ALU.add), r=[r0], w=[ang])
            for dst, src in ((ang, ang), (posf, r0)):
                A("dve", lambda e, src=src: e.tensor_scalar(out=msk[:, :], in0=src[:, :], scalar1=PI, scalar2=None,
                                                            op0=ALU.is_gt), r=[src], w=[msk])
                A("dve", lambda e, dst=dst, src=src: e.scalar_tensor_tensor(
                    out=dst[:, :], in0=msk[:, :], scalar=-2 * PI, in1=src[:, :], op0=ALU.mult, op1=ALU.add),
                  r=[msk, src], w=[dst])

        def ph3():
            A("act", lambda e: e.activation(out=cosT[:, :], in_=ang[:, :], func=AF.Sin), r=[ang], w=[cosT])
            A("act", lambda e: e.activation(out=sinT[:, :], in_=posf[:, :], func=AF.Sin,
                                            scale=vecs[0:64, V_SGN:V_SGN + 1]), r=[posf, vecs], w=[sinT])

        return (cosT, sinT), [ph1, ph2, ph3]

    def rope_tables(b, j):
        tabs, phases = rope_tables_steps(b, j)
        for ph in phases:
            ph()
        return tabs

    def rope_apply(pr, psw, dst_buf, dst_ap):
        cosT, sinT = ropecur["t"]
        t1 = ftp.next()
        t2 = ftp.next()
        A("dve", lambda e: e.tensor_tensor(out=t1[0:64, :], in0=pr[0:64, 0:T], in1=cosT[:, :], op=ALU.mult),
          r=[pr, cosT], w=[t1])
        A("dve", lambda e: e.tensor_tensor(out=t2[0:64, :], in0=psw[0:64, 0:T], in1=sinT[:, :], op=ALU.mult),
          r=[psw, sinT], w=[t2])
        A("dve", lambda e: e.tensor_tensor(out=dst_ap, in0=t1[0:64, :], in1=t2[0:64, :], op=ALU.add),
          r=[t1, t2], w=[dst_buf])

    def lat_norm(chunks, pstat, nch, gcol, dstT, dim):
        ms = ftp.next()
        A("act", lambda e: e.activation(out=ms[:, :], in_=pstat[:, 0:T], func=AF.Ln, scale=1.0 / dim, bias=EPS6),
          r=[pstat, epsc], w=[ms])
        A("act", lambda e: e.activation(out=ms[:, :], in_=ms[:, :], func=AF.Exp, scale=-0.5), r=[ms], w=[ms])
        for c in range(nch):
            A("dve", lambda e, c=c: e.scalar_tensor_tensor(out=dstT[:, c, :], in0=chunks[c][:, :],
                                                           scalar=vecs[:, gcol + c:gcol + c + 1], in1=ms[:, :],
                                                           op0=ALU.mult, op1=ALU.mult),
              r=[chunks[c], vecs, ms], w=[dstT])

    def attn_scores(h, p, qn, qr):
        pS = Sbank.next()
        for cc in range(2):
            off = cc * 128
            mm(pS, pS[:, cc * T:(cc + 1) * T], KTb[h][p][:, off:off + 128], qn[:, :], cc == 0, False,
               [KTb[h][p], qn], tag="S")
            mm(pS, pS[:, cc * T:(cc + 1) * T], kro[p][:, off:off + 128], qr[:, :], False, True,
               [kro[p], kro_all, qr], tag="S")
        return pS

    def attn_step(h, p, j, pS, pod):
        sc = float(192 ** -0.5)
        npair = j + 1
        et = etp.next()
        A("act", lambda e: e.activation(out=et[:, :], in_=pS[:, 0:2 * T], func=AF.Exp, scale=sc), r=[pS], w=[et])
        if p == j:
            A("dve", lambda e: e.tensor_tensor(out=et[:, :], in0=et[:, :],
                                               in1=masks.t.rearrange("p a b -> p (a b)"), op=ALU.mult),
              r=[et, masks], w=[et])
        for cc in range(2):
            mm(pod, pod[:, 0:T], Vb[p][:, cc, h * 128:(h + 1) * 128], et[:, cc * T:(cc + 1) * T],
               p == 0 and cc == 0, p == npair - 1 and cc == 1, [Vb[p], et], tag="PV")
        for cc in range(2):
            mm(pod, pod[:, T:2 * T], ones[:, :], et[:, cc * T:(cc + 1) * T], False,
               p == npair - 1 and cc == 1, [ones, et], tag="den")

    def l1_tile(b, j, tix, pre, nxt_a, nxt_b, dst_d, next_bj):
        pp = cur["pp"]
        xcs, uT = pre
        nst = None
        nsteps = None
        nxt = None
        if ropecur.get("key") != (b, j):
            ropecur["t"] = rope_tables(b, j)
            ropecur["key"] = (b, j)
        ropenext = None
        pst_q = podp.next()
        lag = None
        for i in range(2):
            slot, sv = WS.get(f"cq{i}")
            for hh in range(2):
                c = 2 * i + hh
                pb = pp.next()
                fm_proj(pb, 128, slot, sv, hh * 128, uT, 8)
                A("act", lambda e, pb=pb, c=c: e.activation(out=cq[c][:, :], in_=pb[:, 0:T], func=AF.Copy),
                  r=[pb], w=[cq[c]])
                sq = btp.next()
                A("act", lambda e, pb=pb, sq=sq: e.activation(out=sq[:, :], in_=pb[:, 0:T], func=AF.Square),
                  r=[pb], w=[sq])
                if lag is not None:
                    lag()
                lag = (lambda sq=sq, c=c: mm(pst_q, pst_q[:, 0:T], ones[:, :], sq[:, :], c == 0, c == 3, [ones, sq]))
        pst_k = podp.next()
        slot, sv = WS.get("ckv")
        for c in range(2):
            pb = pp.next()
            fm_proj(pb, 128, slot, sv, c * 128, uT, 8)
            A("act", lambda e, pb=pb, c=c: e.activation(out=ckv[c][:, :], in_=pb[:, 0:T], func=AF.Copy),
              r=[pb], w=[ckv[c]])
            sq = btp.next()
            A("act", lambda e, pb=pb, sq=sq: e.activation(out=sq[:, :], in_=pb[:, 0:T], func=AF.Square),
              r=[pb], w=[sq])
            lag()
            lag = (lambda sq=sq, c=c: mm(pst_k, pst_k[:, 0:T], ones[:, :], sq[:, :], c == 0, c == 1, [ones, sq]))
            if c == 0:
                lat_norm(cq, pst_q, 4, V_QG, cqnT, 512)
        slot, sv = WS.get("kr")
        pkr = pp.next()
        fm_proj(pkr, 64, slot, sv, 0, uT, 8)
        lag()
        lat_norm(ckv, pst_k, 2, V_KVG, ckvnT, 256)
        pks = pp.next()
        fm_proj(pks, 64, slot, sv, 64, uT, 8)
        rope_apply(pkr, pks, kro[j], kro[j][0:64, :])
        for i in range(2):
            slot, sv = WS.get(f"qm1_{i}")
            for hh in range(2):
                h = 2 * i + hh
                pq = pp.next()
                fm_proj(pq, 128, slot, sv, hh * 128, uT, 8)
                A("act", lambda e, pq=pq, h=h: e.activation(out=qmT[h][:, :], in_=pq[:, 0:T], func=AF.Copy),
                  r=[pq], w=[qmT[h]])
        for i in range(2):
            slot, sv = WS.get(f"uk{i}")
            for hl in range(6):
                h = 6 * i + hl
                pb = pp.next()
                fm_proj(pb, 128, slot, sv, hl * 128, ckvnT, 2, tag="kv")
                kb = KTb[h][j]
                if hl % 2 == 0:
                    A("act", lambda e, pb=pb, kb=kb: e.activation(out=kb[:, :], in_=pb[:, 0:T], func=AF.Copy),
                      r=[pb], w=[kb])
                else:
                    A("dve", lambda e, pb=pb, kb=kb: e.tensor_copy(out=kb[:, :], in_=pb[:, 0:T]), r=[pb], w=[kb])
        for i in range(2):
            slot, sv = WS.get(f"uv{i}")
            for tc in range(NTC):
                for (c0, n) in ((0, 512), (512, 256)):
                    pb = pp.next()
                    for kc in range(2):
                        mm(pb, pb[:, 0:n], ckvnT[:, kc, tc * 128:(tc + 1) * 128], sv[:, kc, c0:c0 + n],
                           kc == 0, kc == 1, [slot, ckvnT], tag="kv")
                    A("dve", lambda e, pb=pb, tc=tc, i=i, c0=c0, n=n: e.tensor_copy(
                        out=Vb[j][:, tc, i * 768 + c0:i * 768 + c0 + n], in_=pb[:, 0:n]), r=[pb], w=[Vb[j]])

        def zgroup(i):
            slot, sv = WS.get(f"z1_{i}")
            for hh in range(2):
                pz = pp.next()
                fm_proj(pz, 128, slot, sv, hh * 128, uT, 8)
                zchunk(pz, sz[2 * i + hh])

        zgroup(6)
        zgroup(7)
        mem_attn_all(b, Sbank, podp)

        uq = {}

        def qproj(h):
            i, hh = h // 2, h % 2
            if hh == 0:
                uq["slot"], uq["sv"] = WS.get(f"uq{i}")
            slot, sv = uq["slot"], uq["sv"]
            base = hh * 256
            pqn = pp.next()
            fm_proj(pqn, 128, slot, sv, base, cqnT, 4, tag="q")
            pqr = pp.next()
            fm_proj(pqr, 64, slot, sv, base + 128, cqnT, 4, tag="q")
            pqs = pp.next()
            fm_proj(pqs, 64, slot, sv, base + 192, cqnT, 4, tag="q")
            qn = qnp.next()
            A("act", lambda e: e.activation(out=qn[:, :], in_=pqn[:, 0:T], func=AF.Copy), r=[pqn], w=[qn])
            qr = qrp.next()
            rope_apply(pqr, pqs, qr, qr[0:64, :])
            return qn, qr

        npair = j + 1
        Q = {0: qproj(0)}
        pods = {}
        steps = [(h, p) for h in range(12) for p in range(npair)]
        pSq = {}
        issued = 0
        for si, (h, p) in enumerate(steps):
            if p == 0:
                if h + 1 < 12:
                    Q[h + 1] = qproj(h + 1)
                if h % 2 == 0:
                    zgroup(h // 2)
                pods[h] = podp.next()
            while issued < len(steps) and issued <= si + 2 and steps[issued][0] <= h + 1:
                hh_, pp_ = steps[issued]
                pSq[issued] = attn_scores(hh_, pp_, *Q[hh_])
                issued += 1
            attn_step(h, p, j, pSq.pop(si), pods[h])
            if p == npair - 1:
                softmax_finish(pods[h], sz[h])
                if h == 0 and nxt_a is not None:
                    nst, nsteps = nxt_a()
                if nst is not None and nsteps:
                    nsteps.pop(0)()
                if h == 7 and nst is not None:
                    while nsteps:
                        nsteps.pop(0)()
                    nxt = nxt_b(nst)
                if h == 8 and next_bj is not None:
                    ropenext, rsteps = rope_tables_steps(*next_bj)
                if h in (8, 9, 10) and ropenext is not None:
                    rsteps.pop(0)()
        out_proj(1, xcs, dst_d, b, j, final=True, gorder=(6, 7, 0, 1, 2, 3, 4, 5))
        if ropenext is not None:
            ropecur["t"] = ropenext
            ropecur["key"] = next_bj
        return nxt

    with nc.allow_low_precision("bf16 matmul operands, fp32 accumulation"):
        prep_todo = []
        for l in layers:
            gl = weight_groups(l)
            seen = []
            for gid in mem_order(l) + tile_order(l):
                if gid not in seen:
                    seen.append(gid)
            assert set(seen) == set(gl.keys())
            prep_todo += seen
        prepped = set()

        def bg_prep():
            if prep_todo:
                gid = prep_todo.pop(0)
                prepped.add(gid)
                prep(gid)

        def need_prepped(gid):
            while gid not in prepped:
                bg_prep()

        WS.need = need_prepped
        for li, l in enumerate(layers):
            cur["pp"] = pp_l[l]
            cur["layer"] = l
            src_d = x_d if li == 0 else h1_d
            dst_d = out_d if li == len(layers) - 1 else h1_d
            tiles = [(b, j) for b in range(nseq) for j in range(ntile)]
            dstate["built"] = 0
            dstate["total"] = len(tiles) * 12 if l == 0 else 0
            for b in range(nseq):
                mem_prep(l, b)
            pre = tile_pre_b(l, tile_pre_a(l, src_d, *tiles[0]))
            if l == 1:
                ropecur["t"] = rope_tables(*tiles[0])
                ropecur["key"] = tiles[0]
                while prep_todo:
                    bg_prep()
                S.barrier()
                A("pool", lambda e: e.memset(arena[64:128, koff:koff + SEQ], 0.0), w=[kro_all])
                for qb in qr_bufs:
                    A("pool", lambda e, qb=qb: e.memset(qb[:, :], 0.0), w=[qb])
            for tix, (b, j) in enumerate(tiles):
                if tix + 1 < len(tiles):
                    nb_, nj_ = tiles[tix + 1]
                    nxt_a = (lambda l=l, src_d=src_d, nb_=nb_, nj_=nj_: tile_pre_steps(l, src_d, nb_, nj_))
                    nxt_b = (lambda st, l=l: tile_pre_b(l, st))
                else:
                    nxt_a = nxt_b = None
                if l == 0:
                    pre = l0_tile(b, j, tix, pre, nxt_a, nxt_b, dst_d)
                else:
                    pre = l1_tile(b, j, tix, pre, nxt_a, nxt_b, dst_d,
                                  tiles[tix + 1] if tix + 1 < len(tiles) else None)
        S.emit()
    return nc, []


def host_layout(inp, layers=(0, 1)):
    f = lambda a: np.ascontiguousarray(np.asarray(a, dtype=np.float32))
    col = lambda v, n: f(np.asarray(v, np.float32).reshape(n, 128).T)
    vecs = np.zeros((128, NV), np.float32)
    vecs[:, 0:8] = col(inp["norm_g"][0], 8)
    vecs[:, 8:16] = col(inp["norm_g"][1], 8)
    vecs[:, 16:24] = col(inp["mem_norm_g"][0], 8)
    vecs[:, 24:32] = col(inp["mem_norm_g"][1], 8)
    vecs[:, 32:44] = col(inp["conv_dw_b"][0], 12)
    vecs[:, 44:56] = col(inp["conv_ln_g"][0], 12)
    vecs[:, 56:68] = col(inp["conv_ln_b"][0], 12)
    vecs[:, 68:72] = col(inp["mla_q_norm_g"][0], 4)
    vecs[:, 72:74] = col(inp["mla_kv_norm_g"][0], 2)
    invf = (1.0 / (np.float32(10000.0) ** (np.arange(0, 64, 2, dtype=np.float32) / np.float32(64)))).astype(np.float32)
    vecs[0:64, 74] = np.concatenate([invf, invf])
    vecs[0:64, 75] = np.concatenate([-np.ones(32, np.float32), np.ones(32, np.float32)])
    dw = np.asarray(inp["conv_dw"][0], np.float32)
    vecs[:, 76:76 + 372] = dw.reshape(31, 12, 128).transpose(2, 1, 0).reshape(128, 372)
    cst = np.zeros((128, 128 + 2 * T), np.float32)
    cst[:, 0:128] = np.eye(128, dtype=np.float32)
    p = np.arange(128)[:, None]
    q = np.arange(T)[None, :]
    for r in range(2):
        cst[:, 128 + r * T:128 + (r + 1) * T] = (q >= r * 128 + p).astype(np.float32)
    shared = {"vecs": vecs, "cst": cst, "fg": f(inp["final_norm_g"]).reshape(1, D),
              "wout": f(inp["w_out"]), "wmem": f(inp["w_mem_kv"])}
    if 0 in layers:
        w = np.asarray(inp["conv_w_in"][0], np.float32)
        a, g_, qm, z = w[:, 0:1536], w[:, 1536:3072], w[:, 3072:3584], w[:, 3584:5632]
        ag = np.stack([a.reshape(1024, 12, 128), g_.reshape(1024, 12, 128)], axis=2).reshape(1024, 3072)
        shared["w0"] = f(np.concatenate([ag, z, qm], axis=1))
    if 1 in layers:
        w = np.asarray(inp["mla_w_in"][0], np.float32)
        kr = w[:, 768:832]
        kr_sw = np.concatenate([kr[:, 32:64], kr[:, 0:32]], axis=1)
        shared["w1"] = f(np.concatenate([w[:, 0:768], kr, kr_sw, w[:, 832:1344], w[:, 1344:3392]], axis=1))
        uq = np.asarray(inp["mla_w_uq"][0], np.float32).reshape(512, 12, 192)
        qn, qr = uq[:, :, 0:128], uq[:, :, 128:192]
        qsw = np.concatenate([qr[:, :, 32:64], qr[:, :, 0:32]], axis=2)
        shared["wuq"] = f(np.concatenate([qn, qr, qsw], axis=2).reshape(512, 3072))
        ukv = np.asarray(inp["mla_w_ukv"][0], np.float32).reshape(256, 12, 256)
        shared["wuk"] = f(ukv[:, :, 0:128].reshape(256, 1536))
        shared["wuv"] = f(ukv[:, :, 128:256].reshape(256, 1536))
    return shared


_PROG_CACHE = {}


def _get_prog(layers):
    if layers not in _PROG_CACHE:
        _PROG_CACHE[layers] = build_program(layers)[0]
    return _PROG_CACHE[layers]


def run_layers(inp, xin, layers):
    shared = host_layout(inp, layers)
    mem = np.asarray(inp["mem"], np.float32)
    pos = np.asarray(inp["positions"], np.int32)
    nc = _get_prog(layers)
    in_maps = []
    for c in range(NCORES):
        m = dict(shared)
        m["x"] = np.ascontiguousarray(xin[c * NB:(c + 1) * NB])
        m["mem"] = np.ascontiguousarray(mem[c * NB:(c + 1) * NB])
        m["pos"] = np.ascontiguousarray(pos[c * NB:(c + 1) * NB])
        in_maps.append(m)
    res = run_bass_kernel_spmd(nc, in_maps, core_ids=list(range(NCORES)))
    return np.concatenate([r["out"] for r in res.results], axis=0)


FUSED = True


def kernel(**inp):
    x = np.asarray(inp["x"], np.float32)
    if FUSED:
        return run_layers(inp, x, (0, 1)).astype(np.float32)
    h1 = run_layers(inp, x, (0,))
    return run_layers(inp, h1, (1,)).astype(np.float32)
```

```python
import contextlib
import numpy as np
import concourse.bass as bass
import concourse.mybir as mybir
from concourse.bass_utils import run_bass_kernel_spmd

F32 = mybir.dt.float32
BF16 = mybir.dt.bfloat16
I32 = mybir.dt.int32
AF = mybir.ActivationFunctionType
ALU = mybir.AluOpType

ENGS = ("pe", "act", "dve", "pool", "sp")
KR = 8
KD = 8

NCORES = 8
NB = 2
SEQ = 2048
D = 1024
T = 256
NTC = T // 128
NTILE = SEQ // T
MEM = 256
NV = 448
PI = float(np.pi)


class Buf:
    __slots__ = ("name", "t", "lw", "rd", "const")

    def __init__(self, name, t, const=False):
        self.name = name
        self.t = t
        self.lw = None
        self.rd = []
        self.const = const

    def __getitem__(self, k):
        return self.t[k]


class Op:
    __slots__ = ("eng", "fn", "idx", "dma", "sig", "waits", "clk", "didx", "signo")

    def __init__(self, eng, fn, dma):
        self.eng = eng
        self.fn = fn
        self.dma = dma
        self.sig = False
        self.waits = []
        self.clk = None
        self.didx = -1
        self.signo = -1


class Sched:
    def __init__(self, nc):
        self.nc = nc
        self.ops = {e: [] for e in ENGS}
        self.ndma = {e: 0 for e in ENGS}
        self.clock = {e: {f: -1 for f in ENGS} for e in ENGS}
        self.dma_seen = {e: set() for e in ENGS}
        self.dma_ops = {e: [] for e in ENGS}
        self.same_engine_sync = True

    def add(self, eng, fn, reads=(), writes=(), dma=False):
        op = Op(eng, fn, dma)
        deps = []
        wset = set(id(b) for b in writes)
        for b in reads:
            if b.lw is not None:
                deps.append(b.lw)
        for b in writes:
            if b.lw is not None:
                deps.append(b.lw)
            deps.extend(b.rd)
        for b in writes:
            b.lw = op
            b.rd = []
        for b in reads:
            if id(b) not in wset and not b.const:
                b.rd.append(op)
        self._resolve(op, deps)
        op.idx = len(self.ops[eng])
        self.ops[eng].append(op)
        if dma:
            op.didx = self.ndma[eng]
            self.ndma[eng] += 1
            self.dma_ops[eng].append(op)
        return op

    def _resolve(self, op, deps):
        eng = op.eng
        clk = self.clock[eng]
        best = {}
        for d in deps:
            if d.dma:
                if d not in self.dma_seen[eng]:
                    self.dma_seen[eng].add(d)
                    op.waits.append(d)
                continue
            f = d.eng
            if f == eng and not op.dma:
                if eng == "pe" or not self.same_engine_sync:
                    continue
            if clk[f] >= d.idx:
                continue
            if f not in best or best[f].idx < d.idx:
                best[f] = d
        for f, d in best.items():
            d.sig = True
            op.waits.append(d)
            for g, v in d.clk.items():
                if clk[g] < v:
                    clk[g] = v
            if clk[f] < d.idx:
                clk[f] = d.idx
        if not op.dma:
            op.clk = dict(clk)

    def barrier(self):
        lasts = []
        for e in ENGS:
            comp = [o for o in self.ops[e] if not o.dma and o.fn is not None]
            if comp:
                lasts.append(comp[-1])
            lasts.extend(self.dma_ops[e][-KD:])
        for e in ENGS:
            op = Op(e, None, False)
            self._resolve(op, list(lasts))
            op.idx = len(self.ops[e])
            self.ops[e].append(op)

    def emit(self):
        nc = self.nc
        with contextlib.ExitStack() as st:
            csem = {e: [st.enter_context(nc.semaphore(f"c_{e}_{i}")) for i in range(KR)] for e in ENGS}
            dsem = {e: [st.enter_context(nc.semaphore(f"d_{e}_{i}")) for i in range(KD)]
                    for e in ENGS if self.ndma[e] > 0}
            for e in ENGS:
                n = 0
                for o in self.ops[e]:
                    if o.sig and not o.dma:
                        o.signo = n
                        n += 1

            def wait_for(engobj, d):
                if d.dma:
                    engobj.wait_ge(dsem[d.eng][d.didx % KD], 16 * (d.didx // KD + 1))
                else:
                    engobj.wait_ge(csem[d.eng][d.signo % KR], d.signo // KR + 1)

            def run(e):
                def body(engobj):
                    for o in self.ops[e]:
                        for d in o.waits:
                            wait_for(engobj, d)
                        if o.fn is None:
                            continue
                        if o.dma:
                            m = o.didx
                            if m >= KD:
                                engobj.wait_ge(dsem[e][m % KD], 16 * (m // KD))
                            o.fn(engobj).then_inc(dsem[e][m % KD], 16)
                        else:
                            ins = o.fn(engobj)
                            if o.sig:
                                ins.then_inc(csem[e][o.signo % KR], 1)
                    nd = self.ndma[e]
                    for i in range(min(KD, nd)):
                        m = nd - 1 - i
                        engobj.wait_ge(dsem[e][m % KD], 16 * (m // KD + 1))
                return body

            with nc.Block() as block:
                block.tensor(run("pe"))
                block.scalar(run("act"))
                block.vector(run("dve"))
                block.gpsimd(run("pool"))
                block.sync(run("sp"))


class Ring:
    def __init__(self, items):
        self.items = items
        self.i = 0

    def next(self):
        it = self.items[self.i % len(self.items)]
        self.i += 1
        return it


def weight_groups(layer):
    g = {}
    wm = ("wmem", layer)
    for i in range(2):
        g[f"m{layer}k{i}"] = (wm, 0, 1024, i * 256, 256)
        g[f"m{layer}v{i}"] = (wm, 0, 1024, 512 + i * 256, 256)
    if layer == 0:
        for c in range(12):
            g[f"ag{c}"] = (("w0", None), 0, 1024, c * 256, 256)
        for i in range(8):
            g[f"z0_{i}"] = (("w0", None), 0, 1024, 3072 + i * 256, 256)
        for i in range(2):
            g[f"qm0_{i}"] = (("w0", None), 0, 1024, 5120 + i * 256, 256)
    else:
        for i in range(2):
            g[f"cq{i}"] = (("w1", None), 0, 1024, i * 256, 256)
        g["ckv"] = (("w1", None), 0, 1024, 512, 256)
        g["kr"] = (("w1", None), 0, 1024, 768, 128)
        for i in range(2):
            g[f"qm1_{i}"] = (("w1", None), 0, 1024, 896 + i * 256, 256)
        for i in range(8):
            g[f"z1_{i}"] = (("w1", None), 0, 1024, 1408 + i * 256, 256)
        for i in range(6):
            g[f"uq{i}"] = (("wuq", None), 0, 512, i * 512, 512)
        for i in range(2):
            g[f"uk{i}"] = (("wuk", None), 0, 256, i * 768, 768)
            g[f"uv{i}"] = (("wuv", None), 0, 256, i * 768, 768)
    for i in range(8):
        g[f"o{layer}_{i}"] = (("wout", layer), i * 256, 256, 0, 1024)
    return g


def tile_order(layer):
    if layer == 0:
        o = ["ag0"]
        for c in range(12):
            if c < 11:
                o.append(f"ag{c + 1}")
            if c < 8:
                o.append(f"z0_{c}")
            if c == 9:
                o += [f"qm0_{i}" for i in range(2)]
        return o + [f"o0_{i}" for i in (6, 7, 0, 1, 2, 3, 4, 5)]
    else:
        o = ["cq0", "cq1", "ckv", "kr"] + [f"qm1_{i}" for i in range(2)] + ["uk0", "uk1", "uv0", "uv1"]
        o += ["z1_6", "z1_7", "uq0"]
        for h in range(12):
            if h % 2 == 0 and h // 2 < 6:
                o.append(f"z1_{h // 2}")
            if h + 1 < 12 and (h + 1) % 2 == 0:
                o.append(f"uq{(h + 1) // 2}")
    return o + [f"o{layer}_{i}" for i in (6, 7, 0, 1, 2, 3, 4, 5)]


def mem_order(layer):
    return [f"m{layer}k0", f"m{layer}k1", f"m{layer}v0", f"m{layer}v1"]


def build_program(layers=(0, 1), nseq=NB, ntile=NTILE, dumps=None):
    nc = bass.Bass("TRN2", target_bir_lowering=False)
    S = Sched(nc)
    RS = 5

    def dram(name, shape, dt, kind):
        return nc.dram_tensor(name, list(shape), dt, kind=kind)

    x_d = dram("x", [NB, SEQ, D], F32, "ExternalInput")
    mem_d = dram("mem", [NB, MEM, D], F32, "ExternalInput")
    pos_d = dram("pos", [NB, SEQ], I32, "ExternalInput")
    vecs_d = dram("vecs", [128, NV], F32, "ExternalInput")
    fg_d = dram("fg", [1, D], F32, "ExternalInput")
    cst_d = dram("cst", [128, 128 + 2 * T], F32, "ExternalInput")
    wsrc = {}
    if 0 in layers:
        wsrc["w0"] = dram("w0", [1024, 5632], F32, "ExternalInput")
    if 1 in layers:
        wsrc["w1"] = dram("w1", [1024, 3456], F32, "ExternalInput")
        wsrc["wuq"] = dram("wuq", [512, 3072], F32, "ExternalInput")
        wsrc["wuk"] = dram("wuk", [256, 1536], F32, "ExternalInput")
        wsrc["wuv"] = dram("wuv", [256, 1536], F32, "ExternalInput")
    wsrc["wout"] = dram("wout", [2, 2048, 1024], F32, "ExternalInput")
    wsrc["wmem"] = dram("wmem", [2, 1024, 1024], F32, "ExternalInput")
    out_d = dram("out", [NB, SEQ, D], F32, "ExternalOutput")
    h1_d = nc.dram_tensor("h1s", [NB, SEQ, D], F32) if len(layers) == 2 else None

    def sb(name, shape, dt, const=False):
        return Buf(name, nc.alloc_sbuf_tensor("s_" + name, list(shape), dt), const)

    def A(eng, fn, r=(), w=(), dma=False):
        return S.add(eng, fn, reads=r, writes=w, dma=dma)

    vecs = sb("vecs", [128, NV], F32, const=True)
    ident = sb("ident", [128, 128], BF16, const=True)
    ones = sb("ones", [128, 128], BF16, const=True)
    masks = sb("masks", [128, 2, T], BF16, const=True)
    epsc = sb("epsc", [128, 2], F32, const=True)
    fgt = sb("fgt", [128, D], F32, const=True)
    A("sp", lambda e: e.dma_start(out=vecs[:, :], in_=vecs_d[:, :]), w=[vecs], dma=True)
    A("sp", lambda e: e.dma_start(out=fgt[:, :], in_=fg_d[0:1, :].partition_broadcast(128)), w=[fgt], dma=True)
    A("pool", lambda e: e.memset(ones[:, :], 1.0), w=[ones])
    A("pool", lambda e: e.memset(epsc[:, 0:1], 1e-6), w=[epsc])
    A("pool", lambda e: e.memset(epsc[:, 1:2], 1e-5), w=[epsc])
    EPS6 = epsc[:, 0:1]
    EPS5 = epsc[:, 1:2]
    V_NG = {0: 0, 1: 8}
    V_MG = {0: 16, 1: 24}
    V_DWB, V_LNG, V_LNB, V_QG, V_KVG, V_INVF, V_SGN, V_DW = 32, 44, 56, 68, 72, 74, 75, 76

    P = [Buf(f"P{i}", nc.alloc_psum_tensor(f"ps_P{i}", [128, 512], F32)) for i in range(8)]
    pT = P[7]
    pTv = P[7].t[:, :].bitcast(BF16)
    pp_l = {0: Ring(P[0:4]), 1: Ring(P[0:3])}
    cur = {"pp": pp_l[layers[0]], "layer": layers[0]}
    convp = Ring(P[4:6])
    Sbank = Ring([P[3], P[4], P[7]])
    podp = Ring(P[5:7])

    ring = [sb(f"wr{i}", [128, 2048], BF16) for i in range(RS)]
    KmTs = [sb(f"KmT{b}", [128, 4, MEM], BF16) for b in range(NB)]
    Vms = [sb(f"Vm{b}", [128, 2, 512], BF16) for b in range(NB)]
    xpool = Ring([sb(f"xc{i}", [128, D], F32) for i in range(4)])
    cstf = xpool.next()
    A("sp", lambda e: e.dma_start(out=cstf[:, 0:128 + 2 * T], in_=cst_d[:, :]), w=[cstf], dma=True)
    A("dve", lambda e: e.tensor_copy(out=ident[:, :], in_=cstf[:, 0:128]), r=[cstf], w=[ident])
    A("dve", lambda e: e.tensor_copy(out=masks.t.rearrange("p a b -> p (a b)"), in_=cstf[:, 128:128 + 2 * T]),
      r=[cstf], w=[masks])
    ubp = Ring([sb(f"ub{i}", [128, D], BF16) for i in range(2)])
    ss = sb("ss", [128, 4], F32)
    rstd = sb("rstd", [128, 4], F32)
    ss2 = sb("ss2", [128, 4], F32)
    rstd2 = sb("rstd2", [128, 4], F32)
    uTs = Ring([sb(f"uT{i}", [128, 8, T], BF16) for i in range(2)])
    sz_t = nc.alloc_sbuf_tensor("s_sz", [128, 16, T], BF16)
    sz = [Buf(f"sz{i}", sz_t[:, i, :]) for i in range(16)]
    qm_t = nc.alloc_sbuf_tensor("s_qmT", [128, 4, T], BF16)
    qmT = [Buf(f"qm{i}", qm_t[:, i, :]) for i in range(4)]
    etp = Ring([sb(f"et{i}", [128, 2 * T], BF16) for i in range(4)])
    ftp = Ring([sb(f"ft{i}", [128, T], F32) for i in range(6)])
    btp = Ring([sb(f"bt{i}", [128, T], BF16) for i in range(4)])

    ARENA = 51200
    arena = nc.alloc_sbuf_tensor("s_arena", [128, ARENA], BF16)
    KTb = [[Buf(f"KT{h}_{j}", arena[:, h * SEQ + j * T: h * SEQ + (j + 1) * T]) for j in range(NTILE)]
           for h in range(12)]
    voff = 12 * SEQ
    Vb = [Buf(f"V{j}", arena[:, voff + j * NTC * 1536: voff + (j + 1) * NTC * 1536]
              .rearrange("p (c n) -> p c n", c=NTC)) for j in range(NTILE)]
    koff = voff + 16 * 1536
    kro = [Buf(f"kro{j}", arena[:, koff + j * T: koff + (j + 1) * T]) for j in range(NTILE)]
    kro_all = Buf("kro_all", arena[:, koff:koff + SEQ])
    assert koff + SEQ <= ARENA
    NSTG = 2
    stg32 = [Buf(f"stg32_{i}", arena[:, i * 6144: i * 6144 + 4096].bitcast(F32)) for i in range(NSTG)]
    stg16 = [Buf(f"stg16_{i}", arena[:, i * 6144 + 4096: i * 6144 + 6144]) for i in range(NSTG)]
    o0 = NSTG * 6144
    GW = 30 + T
    glu = [Buf(f"glu{c}", arena[:, o0 + c * GW: o0 + (c + 1) * GW]) for c in range(12)]
    o1 = o0 + 12 * GW
    cvb = [Buf(f"cvb{c}", arena[:, o1 + c * T: o1 + (c + 1) * T]) for c in range(12)]
    o2 = o1 + 12 * T
    NDG = 4
    KLO = 17
    diag_lo = [Buf(f"diaglo{i}", arena[:, o2 + i * 31 * 128: o2 + i * 31 * 128 + KLO * 128]
                   .rearrange("p (k c) -> p k c", k=KLO)) for i in range(NDG)]
    diag_hi = [Buf(f"diaghi{i}", arena[:, o2 + i * 31 * 128 + KLO * 128: o2 + (i + 1) * 31 * 128]
                   .rearrange("p (k c) -> p k c", k=31 - KLO)) for i in range(NDG)]
    o3 = o2 + NDG * 31 * 128
    lnt = [Buf(f"lnt{i}", arena[:, o3 + i * 2 * T: o3 + (i + 1) * 2 * T].bitcast(F32)) for i in range(4)]
    assert o3 + 8 * T <= ARENA

    if 1 in layers:
        cq_t = nc.alloc_sbuf_tensor("s_cq", [128, 4, T], F32)
        cq = [Buf(f"cq{i}", cq_t[:, i, :]) for i in range(4)]
        cqnT = sb("cqnT", [128, 4, T], BF16)
        ckv_t = nc.alloc_sbuf_tensor("s_ckv", [128, 2, T], F32)
        ckv = [Buf(f"ckv{i}", ckv_t[:, i, :]) for i in range(2)]
        ckvnT = sb("ckvnT", [128, 2, T], BF16)
        posi = sb("posi", [64, T], I32)
        rp = [sb(f"rp{i}", [64, T], F32) for i in range(5)]
        kint = sb("kint", [64, T], I32)
        cs_ring = Ring([(sb(f"cosT{i}", [64, T], F32), sb(f"sinT{i}", [64, T], F32)) for i in range(2)])
        ropecur = {}
        qnp = Ring([sb(f"qn{i}", [128, T], BF16) for i in range(2)])
        qr_bufs = [sb(f"qr{i}", [128, T], BF16) for i in range(2)]
        qrp = Ring(qr_bufs)

    groups = {}
    for l in layers:
        groups.update(weight_groups(l))
    scr = {gid: Buf(f"ws_{gid}", nc.dram_tensor(f"ws_{gid}", [128, 2048], BF16)) for gid in groups}
    prep_state = {"n": 0}
    cast_engs = ("dve", "act")

    def gshape(gid):
        _, r0, nr, c0, ncol = groups[gid]
        return nr // 128, ncol

    def prep(gid):
        (sname, sidx), r0, nr, c0, ncol = groups[gid]
        kcn = nr // 128
        cnt = kcn * ncol
        src = wsrc[sname]
        if sidx is None:
            sap = src[r0:r0 + nr, c0:c0 + ncol]
        else:
            sap = src[sidx, r0:r0 + nr, c0:c0 + ncol]
        sap = sap.rearrange("(kc p) c -> p kc c", p=128)
        i = prep_state["n"]
        prep_state["n"] += 1
        s32, s16 = stg32[i % NSTG], stg16[i % NSTG]
        A("pool", lambda e: e.dma_start(out=s32[:, 0:cnt].rearrange("p (k c) -> p k c", k=kcn), in_=sap),
          w=[s32], dma=True)

        def finish():
            A("act", lambda e: e.activation(out=s16[:, 0:cnt], in_=s32[:, 0:cnt], func=AF.Copy), r=[s32], w=[s16])
            A("act", lambda e: e.dma_start(out=scr[gid][:, 0:cnt], in_=s16[:, 0:cnt]), r=[s16], w=[scr[gid]],
              dma=True)
        return finish

    class WStream:
        def __init__(self, order):
            self.order = order
            self.i = 0
            self.issued = 0

        def _issue(self, n):
            gid = self.order[n]
            slot = ring[n % RS]
            kcn, ncol = gshape(gid)
            cnt = kcn * ncol
            self.need(gid)
            assert scr[gid].lw is not None, gid
            A("sp", lambda e: e.dma_start(out=slot[:, 0:cnt], in_=scr[gid][:, 0:cnt]),
              r=[scr[gid]], w=[slot], dma=True)

        def get(self, gid):
            assert self.order[self.i] == gid, (self.order[self.i], gid)
            while self.issued < min(len(self.order), self.i + RS):
                self._issue(self.issued)
                self.issued += 1
            slot = ring[self.i % RS]
            self.i += 1
            kcn, ncol = gshape(gid)
            return slot, slot[:, 0:kcn * ncol].rearrange("p (k c) -> p k c", k=kcn)

    order = []
    for l in layers:
        for b in range(nseq):
            order += mem_order(l)
        for b in range(nseq):
            for j in range(ntile):
                order += tile_order(l)
    WS = WStream(order)

    MMF = {
        "proj": lambda e, o, l, r, st, sp: e.matmul(o, lhsT=l, rhs=r, start=st, stop=sp),
        "conv": lambda e, o, l, r, st, sp: e.matmul(o, lhsT=l, rhs=r, start=st, stop=sp),
        "stat": lambda e, o, l, r, st, sp: e.matmul(o, lhsT=l, rhs=r, start=st, stop=sp, skip_group_check=True),
        "S": lambda e, o, l, r, st, sp: e.matmul(o, lhsT=l, rhs=r, start=st, stop=sp, skip_group_check=True),
        "PV": lambda e, o, l, r, st, sp: e.matmul(o, lhsT=l, rhs=r, start=st, stop=sp, skip_group_check=True),
        "den": lambda e, o, l, r, st, sp: e.matmul(o, lhsT=l, rhs=r, start=st, stop=sp, skip_group_check=True),
        "out": lambda e, o, l, r, st, sp: e.matmul(o, lhsT=l, rhs=r, start=st, stop=sp),
        "mem": lambda e, o, l, r, st, sp: e.matmul(o, lhsT=l, rhs=r, start=st, stop=sp, skip_group_check=True),
        "kv": lambda e, o, l, r, st, sp: e.matmul(o, lhsT=l, rhs=r, start=st, stop=sp),
        "q": lambda e, o, l, r, st, sp: e.matmul(o, lhsT=l, rhs=r, start=st, stop=sp),
    }

    def mm(pbuf, out_ap, lhsT, rhs, start, stop, reads, tag="proj"):
        f = MMF[tag]
        A("pe", lambda e: f(e, out_ap, lhsT, rhs, start, stop), r=reads, w=[pbuf])

    def rsqrt_small(n, ssb, rsb):
        A("act", lambda e: e.activation(out=rsb[:, 0:n], in_=ssb[:, 0:n], func=AF.Ln, scale=1.0 / D, bias=EPS6),
          r=[ssb, epsc], w=[rsb])
        A("act", lambda e: e.activation(out=rsb[:, 0:n], in_=rsb[:, 0:n], func=AF.Exp, scale=-0.5),
          r=[rsb], w=[rsb])

    def norm_A_steps(xcs):
        n = len(xcs)
        ubs = [ubp.next() for _ in range(n)]
        steps = []
        for tc, xc in enumerate(xcs):
            ub = ubs[tc]
            steps.append(lambda xc=xc, ub=ub, tc=tc: A(
                "act", lambda e: e.activation(out=ub[:, :], in_=xc[:, :], func=AF.Square,
                                              accum_out=ss[:, tc:tc + 1]), r=[xc], w=[ub, ss]))
        steps.append(lambda: rsqrt_small(n, ss, rstd))
        for tc, xc in enumerate(xcs):
            ub = ubs[tc]
            steps.append(lambda xc=xc, ub=ub, tc=tc: A(
                "act", lambda e: e.activation(out=ub[:, :], in_=xc[:, :], func=AF.Copy,
                                              scale=rstd[:, tc:tc + 1]), r=[xc, rstd], w=[ub]))
        return ubs, steps

    def norm_A(xcs):
        ubs, steps = norm_A_steps(xcs)
        for st in steps:
            st()
        return ubs

    def tr_banks():
        if cur["layer"] == 0:
            return [P[7], P[2]]
        return [cur["pp"].next(), cur["pp"].next()]

    def norm_B(ubs, gcol0, dstT):
        gap = vecs[:, gcol0:gcol0 + 8].unsqueeze(2).to_broadcast([128, 8, 128])
        banks = tr_banks()
        for tc, ub in enumerate(ubs):
            bank = banks[tc % len(banks)]
            bv = bank.t[:, :].bitcast(BF16)
            for kc in range(8):
                A("pe", lambda e, kc=kc, ub=ub, bv=bv: e.transpose(out=bv[:, kc * 128:(kc + 1) * 128],
                                                                   in_=ub[:, kc * 128:(kc + 1) * 128],
                                                                   identity=ident[:, :]),
                  r=[ub, ident], w=[bank])
            A("dve", lambda e, tc=tc, bv=bv: e.tensor_tensor(out=dstT[:, :, tc * 128:(tc + 1) * 128],
                                                             in0=bv.rearrange("p (k c) -> p k c", k=8), in1=gap,
                                                             op=ALU.mult), r=[bank, vecs], w=[dstT])

    def norm_T(xcs, gcol0, dstT):
        norm_B(norm_A(xcs), gcol0, dstT)

    def fm_proj(pb, m, slot, sv, c0, src, kcn, tag="proj"):
        for kc in range(kcn):
            mm(pb, pb[0:m, 0:T], sv[:, kc, c0:c0 + m], src[:, kc, :], kc == 0, kc == kcn - 1, [slot, src], tag=tag)

    def mem_prep(l, b):
        pp = cur["pp"]
        KmT, Vm = KmTs[b], Vms[b]
        uT = uTs.next()
        xcs = [xpool.next() for _ in range(2)]
        for mc, xc in enumerate(xcs):
            A("sp", lambda e, xc=xc, mc=mc: e.dma_start(out=xc[:, :], in_=mem_d[b, mc * 128:(mc + 1) * 128, :]),
              w=[xc], dma=True)
        norm_T(xcs, V_MG[l], uT)
        for i in range(2):
            slot, sv = WS.get(f"m{l}k{i}")
            for hh in range(2):
                h = 2 * i + hh
                pb = pp.next()
                fm_proj(pb, 128, slot, sv, hh * 128, uT, 8)
                A("act", lambda e, pb=pb, h=h: e.activation(out=KmT[:, h, :], in_=pb[:, 0:MEM], func=AF.Copy),
                  r=[pb], w=[KmT])
        for i in range(2):
            slot, sv = WS.get(f"m{l}v{i}")
            for mc in range(2):
                pb = pp.next()
                for kc in range(8):
                    mm(pb, pb[:, 0:256], uT[:, kc, mc * 128:(mc + 1) * 128], sv[:, kc, :], kc == 0, kc == 7,
                       [slot, uT])
                A("act", lambda e, pb=pb, mc=mc, i=i: e.activation(out=Vm[:, mc, i * 256:(i + 1) * 256],
                                                                  in_=pb[:, 0:256], func=AF.Copy),
                  r=[pb], w=[Vm])

    def zchunk(pz, dst):
        th = btp.next()
        A("act", lambda e: e.activation(out=th[:, :], in_=pz[:, 0:T], func=AF.Tanh, scale=0.5), r=[pz], w=[th])
        A("dve", lambda e: e.scalar_tensor_tensor(out=dst[:, :], in0=th[:, :], scalar=1.0, in1=pz[:, 0:T],
                                                  op0=ALU.add, op1=ALU.mult), r=[th, pz], w=[dst])

    def softmax_finish(pod, dst):
        rden = ftp.next()
        A("dve", lambda e: e.reciprocal(out=rden[:, :], in_=pod[:, T:2 * T]), r=[pod], w=[rden])
        tt = ftp.next()
        A("dve", lambda e: e.scalar_tensor_tensor(out=tt[:, :], in0=pod[:, 0:T], scalar=0.5, in1=rden[:, :],
                                                  op0=ALU.mult, op1=ALU.mult), r=[pod, rden], w=[tt])
        A("pool", lambda e: e.tensor_tensor(out=dst[:, :], in0=dst[:, :], in1=tt[:, :], op=ALU.mult),
          r=[dst, tt], w=[dst])

    def mem_attn_all(b, srings, prings):
        sc = float(128 ** -0.5)
        KmT, Vm = KmTs[b], Vms[b]

        def scores(h):
            pS = srings.next()
            for mc in range(2):
                mm(pS, pS[:, mc * T:(mc + 1) * T], KmT[:, h, mc * 128:(mc + 1) * 128], qmT[h][:, :], mc == 0, True,
                   [KmT, qmT[h]], tag="mem")
            return pS

        q = [scores(0), scores(1)]
        for h in range(4):
            pS = q.pop(0)
            if h + 2 < 4:
                q.append(scores(h + 2))
            et = etp.next()
            pod = prings.next()
            A("act", lambda e, pS=pS, et=et: e.activation(out=et[:, :], in_=pS[:, 0:2 * T], func=AF.Exp, scale=sc),
              r=[pS], w=[et])
            for mc in range(2):
                mm(pod, pod[:, 0:T], Vm[:, mc, h * 128:(h + 1) * 128], et[:, mc * T:(mc + 1) * T], mc == 0, mc == 1,
                   [Vm, et], tag="PV")
            for mc in range(2):
                mm(pod, pod[:, T:2 * T], ones[:, :], et[:, mc * T:(mc + 1) * T], False, mc == 1, [ones, et],
                   tag="den")
            softmax_finish(pod, sz[12 + h])

    dramB = {}

    def dbuf(d, b, j):
        key = (d.name, b, j)
        if key not in dramB:
            dramB[key] = Buf("B_%s_%d_%d" % key, d)
        return dramB[key]

    def load_tile(src_d, b, j):
        xcs = [xpool.next() for _ in range(NTC)]
        for tc, xc in enumerate(xcs):
            r0 = j * T + tc * 128
            A("sp", lambda e, xc=xc, r0=r0: e.dma_start(out=xc[:, :], in_=src_d[b, r0:r0 + 128, :]),
              r=[dbuf(src_d, b, j)], w=[xc], dma=True)
        return xcs

    def tile_pre_a(l, src_d, b, j):
        xcs = load_tile(src_d, b, j)
        return xcs, norm_A(xcs)

    def tile_pre_steps(l, src_d, b, j):
        xcs = load_tile(src_d, b, j)
        ubs, steps = norm_A_steps(xcs)
        return (xcs, ubs), steps

    def tile_pre_b(l, st):
        xcs, ubs = st
        uT = uTs.next()
        norm_B(ubs, V_NG[l], uT)
        return xcs, uT

    def out_proj(l, xcs, dst_d, b, j, final, gorder=(0, 1, 2, 3, 4, 5, 6, 7)):
        acc = [P[0], P[1], P[2], P[3]]
        for gi_, i in enumerate(gorder):
            slot, sv = WS.get(f"o{l}_{i}")
            for kk in range(2):
                kc = 2 * i + kk
                first = (gi_ == 0 and kk == 0)
                last = (gi_ == 7 and kk == 1)
                for tc in range(NTC):
                    for half in range(2):
                        pb = acc[tc * 2 + half]
                        mm(pb, pb[:, 0:512], sz[kc][:, tc * 128:(tc + 1) * 128],
                           sv[:, kk, half * 512:(half + 1) * 512], first, last, [slot, sz[kc]], tag="out")
        for tc, xc in enumerate(xcs):
            for half in range(2):
                pb = acc[tc * 2 + half]
                A("dve", lambda e, xc=xc, pb=pb, half=half: e.tensor_tensor(
                    out=xc[:, half * 512:(half + 1) * 512], in0=pb[:, 0:512],
                    in1=xc[:, half * 512:(half + 1) * 512], op=ALU.add), r=[pb, xc], w=[xc])
        if final:
            for tc, xc in enumerate(xcs):
                ub = ubp.next()
                A("act", lambda e, xc=xc, ub=ub, tc=tc: e.activation(out=ub[:, :], in_=xc[:, :], func=AF.Square,
                                                                    accum_out=ss2[:, tc:tc + 1]),
                  r=[xc], w=[ub, ss2])
            rsqrt_small(NTC, ss2, rstd2)
            for tc, xc in enumerate(xcs):
                A("dve", lambda e, xc=xc, tc=tc: e.scalar_tensor_tensor(
                    out=xc[:, :], in0=xc[:, :], scalar=rstd2[:, tc:tc + 1], in1=fgt[:, :],
                    op0=ALU.mult, op1=ALU.mult), r=[xc, rstd2, fgt], w=[xc])
        dstB = dbuf(dst_d, b, j)
        for tc, xc in enumerate(xcs):
            r0 = j * T + tc * 128
            A("pool", lambda e, xc=xc, r0=r0: e.dma_start(out=dst_d[b, r0:r0 + 128, :], in_=xc[:, :]),
              r=[xc], w=[dstB], dma=True)

    dstate = {"built": 0, "total": 0}

    def ensure_diag(g):
        while dstate["built"] <= min(g, dstate["total"] - 1):
            gi = dstate["built"]
            c = gi % 12
            dl, dh = diag_lo[gi % NDG], diag_hi[gi % NDG]
            w0_ = V_DW + c * 31
            A("pool", lambda e, dl=dl, w0_=w0_: e.tensor_tensor(
                out=dl[:, :, :], in0=ident[:, :].unsqueeze(1).to_broadcast([128, KLO, 128]),
                in1=vecs[:, w0_:w0_ + KLO].unsqueeze(2).to_broadcast([128, KLO, 128]),
                op=ALU.mult), r=[ident, vecs], w=[dl])
            A("dve", lambda e, dh=dh, w0_=w0_: e.tensor_tensor(
                out=dh[:, :, :], in0=ident[:, :].unsqueeze(1).to_broadcast([128, 31 - KLO, 128]),
                in1=vecs[:, w0_ + KLO:w0_ + 31].unsqueeze(2).to_broadcast([128, 31 - KLO, 128]),
                op=ALU.mult), r=[ident, vecs], w=[dh])
            dstate["built"] += 1

    def l0_tile(b, j, tix, pre, nxt_a, nxt_b, dst_d):
        pp = cur["pp"]
        xcs, uT = pre
        nst = None
        nsteps = None
        nxt = None
        s12 = P[6]

        def ag(c):
            slot, sv = WS.get(f"ag{c}")
            pa, pg = P[0], P[1]
            fm_proj(pa, 128, slot, sv, 0, uT, 8)
            fm_proj(pg, 128, slot, sv, 128, uT, 8)
            th = ftp.next()
            A("act", lambda e: e.activation(out=th[:, :], in_=pg[:, 0:T], func=AF.Tanh, scale=0.5), r=[pg], w=[th])
            g = glu[c]
            if j == 0:
                A("dve", lambda e: e.memset(g[:, 0:30], 0.0), w=[g])
            A("dve", lambda e: e.scalar_tensor_tensor(out=g[:, 30:30 + T], in0=th[:, :], scalar=1.0,
                                                      in1=pa[:, 0:T], op0=ALU.add, op1=ALU.mult),
              r=[th, pa, g], w=[g])

        def zgroup(i):
            slot, sv = WS.get(f"z0_{i}")
            for hh in range(2):
                pz = P[2 + hh]
                fm_proj(pz, 128, slot, sv, hh * 128, uT, 8)
                zchunk(pz, sz[2 * i + hh])

        def stats(c, sq):
            mm(s12, s12[:, 0:T], ones[:, :], cvb[c][:, :], c == 0, c == 11, [ones, cvb[c]], tag="stat")
            mm(s12, s12[:, T:2 * T], ones[:, :], sq[:, :], False, c == 11, [ones, sq], tag="stat")

        ensure_diag(tix * 12 + 2)
        ag(0)
        prev_sq = None
        for c in range(12):
            if c < 11:
                ag(c + 1)
            g = glu[c]
            gi = tix * 12 + c
            dl, dh = diag_lo[gi % NDG], diag_hi[gi % NDG]
            pc = convp.next()
            for k in range(31):
                if k < KLO:
                    mm(pc, pc[:, 0:T], dl[:, k, :], g[:, k:k + T], k == 0, k == 30, [dl, g], tag="conv")
                else:
                    mm(pc, pc[:, 0:T], dh[:, k - KLO, :], g[:, k:k + T], k == 0, k == 30, [dh, g], tag="conv")
            ensure_diag(gi + 3)
            cv = cvb[c]
            A("act", lambda e, pc=pc, cv=cv, c=c: e.activation(out=cv[:, :], in_=pc[:, 0:T], func=AF.Identity,
                                                              scale=0.5, bias=vecs[:, V_DWB + c:V_DWB + c + 1]),
              r=[pc, vecs], w=[cv])
            sq = btp.next()
            A("act", lambda e, pc=pc, sq=sq, c=c: e.activation(out=sq[:, :], in_=pc[:, 0:T], func=AF.Square,
                                                              scale=0.5, bias=vecs[:, V_DWB + c:V_DWB + c + 1]),
              r=[pc, vecs], w=[sq])
            if prev_sq is not None:
                stats(c - 1, prev_sq)
            prev_sq = sq
            A("dve", lambda e, g=g: e.tensor_copy(out=g[:, 0:30], in_=g[:, T:T + 30]), r=[g], w=[g])
            if c < 8:
                zgroup(c)
            if c == 0 and nxt_a is not None:
                nst, nsteps = nxt_a()
            if nst is not None and nsteps:
                nsteps.pop(0)()
            if c == 8 and nst is not None:
                while nsteps:
                    nsteps.pop(0)()
                nxt = nxt_b(nst)
            if c % 2 == 1:
                bg_prep()
            if c == 9:
                qring = Ring([P[2], P[3], P[7]])
                for i in range(2):
                    slot, sv = WS.get(f"qm0_{i}")
                    for hh in range(2):
                        h = 2 * i + hh
                        pq = qring.next()
                        fm_proj(pq, 128, slot, sv, hh * 128, uT, 8)
                        A("act", lambda e, pq=pq, h=h: e.activation(out=qmT[h][:, :], in_=pq[:, 0:T], func=AF.Copy),
                          r=[pq], w=[qmT[h]])
                mem_attn_all(b, Ring([P[2], P[3], P[7]]), Ring([P[0], P[1]]))
        stats(11, prev_sq)
        mean, msq, rs, nmr = lnt
        A("dve", lambda e: e.tensor_scalar(out=mean[:, :], in0=s12[:, 0:T], scalar1=1.0 / 1536, scalar2=None,
                                           op0=ALU.mult), r=[s12], w=[mean])
        A("dve", lambda e: e.tensor_tensor(out=msq[:, :], in0=mean[:, :], in1=mean[:, :], op=ALU.mult),
          r=[mean], w=[msq])
        A("dve", lambda e: e.scalar_tensor_tensor(out=rs[:, :], in0=s12[:, T:2 * T], scalar=1.0 / 1536,
                                                  in1=msq[:, :], op0=ALU.mult, op1=ALU.subtract),
          r=[s12, msq], w=[rs])
        A("act", lambda e: e.activation(out=rs[:, :], in_=rs[:, :], func=AF.Ln, bias=EPS5), r=[rs, epsc], w=[rs])
        A("act", lambda e: e.activation(out=rs[:, :], in_=rs[:, :], func=AF.Exp, scale=-0.5), r=[rs], w=[rs])
        A("dve", lambda e: e.scalar_tensor_tensor(out=nmr[:, :], in0=mean[:, :], scalar=-1.0, in1=rs[:, :],
                                                  op0=ALU.mult, op1=ALU.mult), r=[mean, rs], w=[nmr])
        def ln_s12(c):
            t1 = ftp.next()
            cv = cvb[c]
            A("dve", lambda e: e.tensor_tensor(out=t1[:, :], in0=cv[:, :], in1=rs[:, :], op=ALU.mult),
              r=[cv, rs], w=[t1])
            A("dve", lambda e: e.tensor_tensor(out=t1[:, :], in0=t1[:, :], in1=nmr[:, :], op=ALU.add),
              r=[t1, nmr], w=[t1])
            sl = btp.next()
            A("act", lambda e: e.activation(out=sl[:, :], in_=t1[:, :], func=AF.Silu,
                                            scale=vecs[:, V_LNG + c:V_LNG + c + 1],
                                            bias=vecs[:, V_LNB + c:V_LNB + c + 1]), r=[t1, vecs], w=[sl])
            return sl

        def ln_s3(c, sl):
            dst = sz[c]
            A("dve", lambda e: e.scalar_tensor_tensor(out=dst[:, :], in0=sl[:, :], scalar=0.5, in1=dst[:, :],
                                                      op0=ALU.mult, op1=ALU.mult), r=[dst, sl], w=[dst])

        sl_prev = ln_s12(0)
        for c in range(1, 12):
            sl_c = ln_s12(c)
            ln_s3(c - 1, sl_prev)
            sl_prev = sl_c
        ln_s3(11, sl_prev)
        out_proj(0, xcs, dst_d, b, j, final=False, gorder=(6, 7, 0, 1, 2, 3, 4, 5))
        return nxt

    def rope_tables_steps(b, j):
        t0 = j * T
        cosT, sinT = cs_ring.next()
        posf, kf, r0, msk, ang = rp
        C1 = 6.28125
        C2 = 2 * PI - C1

        def ph1():
            A("sp", lambda e: e.dma_start(out=posi[:, :], in_=pos_d[b:b + 1, t0:t0 + T].partition_broadcast(64)),
              w=[posi], dma=True)
            A("dve", lambda e: e.tensor_copy(out=posf[:, :], in_=posi[:, :]), r=[posi], w=[posf])
            A("dve", lambda e: e.tensor_scalar(out=posf[:, :], in0=posf[:, :],
                                               scalar1=vecs[0:64, V_INVF:V_INVF + 1], scalar2=None, op0=ALU.mult),
              r=[posf, vecs], w=[posf])
            A("dve", lambda e: e.tensor_scalar(out=kf[:, :], in0=posf[:, :], scalar1=1.0 / (2 * PI), scalar2=None,
                                               op0=ALU.mult), r=[posf], w=[kf])
            A("dve", lambda e: e.tensor_copy(out=kint[:, :], in_=kf[:, :]), r=[kf], w=[kint])
            A("dve", lambda e: e.tensor_copy(out=kf[:, :], in_=kint[:, :]), r=[kint], w=[kf])
            A("dve", lambda e: e.scalar_tensor_tensor(out=r0[:, :], in0=kf[:, :], scalar=-C1, in1=posf[:, :],
                                                      op0=ALU.mult, op1=ALU.add), r=[kf, posf], w=[r0])
            A("dve", lambda e: e.scalar_tensor_tensor(out=r0[:, :], in0=kf[:, :], scalar=-C2, in1=r0[:, :],
                                                      op0=ALU.mult, op1=ALU.add), r=[kf, r0], w=[r0])

        def ph2():
            A("dve", lambda e: e.tensor_scalar(out=ang[:, :], in0=r0[:, :], scalar1=0.5 * PI, scalar2=None,
                                               op0=ALU.add), r=[r0], w=[ang])
            for dst, src in ((ang, ang), (posf, r0)):
                A("dve", lambda e, src=src: e.tensor_scalar(out=msk[:, :], in0=src[:, :], scalar1=PI, scalar2=None,
                                                            op0=ALU.is_gt), r=[src], w=[msk])
                A("dve", lambda e, dst=dst, src=src: e.scalar_tensor_tensor(
                    out=dst[:, :], in0=msk[:, :], scalar=-2 * PI, in1=src[:, :], op0=ALU.mult, op1=ALU.add),
                  r=[msk, src], w=[dst])

        def ph3():
            A("act", lambda e: e.activation(out=cosT[:, :], in_=ang[:, :], func=AF.Sin), r=[ang], w=[cosT])
            A("act", lambda e: e.activation(out=sinT[:, :], in_=posf[:, :], func=AF.Sin,
                                            scale=vecs[0:64, V_SGN:V_SGN + 1]), r=[posf, vecs], w=[sinT])

        return (cosT, sinT), [ph1, ph2, ph3]

    def rope_tables(b, j):
        tabs, phases = rope_tables_steps(b, j)
        for ph in phases:
            ph()
        return tabs

    def rope_apply(pr, psw, dst_buf, dst_ap):
        cosT, sinT = ropecur["t"]
        t1 = ftp.next()
        t2 = ftp.next()
        A("dve", lambda e: e.tensor_tensor(out=t1[0:64, :], in0=pr[0:64, 0:T], in1=cosT[:, :], op=ALU.mult),
          r=[pr, cosT], w=[t1])
        A("dve", lambda e: e.tensor_tensor(out=t2[0:64, :], in0=psw[0:64, 0:T], in1=sinT[:, :], op=ALU.mult),
          r=[psw, sinT], w=[t2])
        A("dve", lambda e: e.tensor_tensor(out=dst_ap, in0=t1[0:64, :], in1=t2[0:64, :], op=ALU.add),
          r=[t1, t2], w=[dst_buf])

    def lat_norm(chunks, pstat, nch, gcol, dstT, dim):
        ms = ftp.next()
        A("act", lambda e: e.activation(out=ms[:, :], in_=pstat[:, 0:T], func=AF.Ln, scale=1.0 / dim, bias=EPS6),
          r=[pstat, epsc], w=[ms])
        A("act", lambda e: e.activation(out=ms[:, :], in_=ms[:, :], func=AF.Exp, scale=-0.5), r=[ms], w=[ms])
        for c in range(nch):
            A("dve", lambda e, c=c: e.scalar_tensor_tensor(out=dstT[:, c, :], in0=chunks[c][:, :],
                                                           scalar=vecs[:, gcol + c:gcol + c + 1], in1=ms[:, :],
                                                           op0=ALU.mult, op1=ALU.mult),
              r=[chunks[c], vecs, ms], w=[dstT])

    def attn_scores(h, p, qn, qr):
        pS = Sbank.next()
        for cc in range(2):
            off = cc * 128
            mm(pS, pS[:, cc * T:(cc + 1) * T], KTb[h][p][:, off:off + 128], qn[:, :], cc == 0, False,
               [KTb[h][p], qn], tag="S")
            mm(pS, pS[:, cc * T:(cc + 1) * T], kro[p][:, off:off + 128], qr[:, :], False, True,
               [kro[p], kro_all, qr], tag="S")
        return pS

    def attn_step(h, p, j, pS, pod):
        sc = float(192 ** -0.5)
        npair = j + 1
        et = etp.next()
        A("act", lambda e: e.activation(out=et[:, :], in_=pS[:, 0:2 * T], func=AF.Exp, scale=sc), r=[pS], w=[et])
        if p == j:
            A("dve", lambda e: e.tensor_tensor(out=et[:, :], in0=et[:, :],
                                               in1=masks.t.rearrange("p a b -> p (a b)"), op=ALU.mult),
              r=[et, masks], w=[et])
        for cc in range(2):
            mm(pod, pod[:, 0:T], Vb[p][:, cc, h * 128:(h + 1) * 128], et[:, cc * T:(cc + 1) * T],
               p == 0 and cc == 0, p == npair - 1 and cc == 1, [Vb[p], et], tag="PV")
        for cc in range(2):
            mm(pod, pod[:, T:2 * T], ones[:, :], et[:, cc * T:(cc + 1) * T], False,
               p == npair - 1 and cc == 1, [ones, et], tag="den")

    def l1_tile(b, j, tix, pre, nxt_a, nxt_b, dst_d, next_bj):
        pp = cur["pp"]
        xcs, uT = pre
        nst = None
        nsteps = None
        nxt = None
        if ropecur.get("key") != (b, j):
            ropecur["t"] = rope_tables(b, j)
            ropecur["key"] = (b, j)
        ropenext = None
        pst_q = podp.next()
        lag = None
        for i in range(2):
            slot, sv = WS.get(f"cq{i}")
            for hh in range(2):
                c = 2 * i + hh
                pb = pp.next()
                fm_proj(pb, 128, slot, sv, hh * 128, uT, 8)
                A("act", lambda e, pb=pb, c=c: e.activation(out=cq[c][:, :], in_=pb[:, 0:T], func=AF.Copy),
                  r=[pb], w=[cq[c]])
                sq = btp.next()
                A("act", lambda e, pb=pb, sq=sq: e.activation(out=sq[:, :], in_=pb[:, 0:T], func=AF.Square),
                  r=[pb], w=[sq])
                if lag is not None:
                    lag()
                lag = (lambda sq=sq, c=c: mm(pst_q, pst_q[:, 0:T], ones[:, :], sq[:, :], c == 0, c == 3, [ones, sq]))
        pst_k = podp.next()
        slot, sv = WS.get("ckv")
        for c in range(2):
            pb = pp.next()
            fm_proj(pb, 128, slot, sv, c * 128, uT, 8)
            A("act", lambda e, pb=pb, c=c: e.activation(out=ckv[c][:, :], in_=pb[:, 0:T], func=AF.Copy),
              r=[pb], w=[ckv[c]])
            sq = btp.next()
            A("act", lambda e, pb=pb, sq=sq: e.activation(out=sq[:, :], in_=pb[:, 0:T], func=AF.Square),
              r=[pb], w=[sq])
            lag()
            lag = (lambda sq=sq, c=c: mm(pst_k, pst_k[:, 0:T], ones[:, :], sq[:, :], c == 0, c == 1, [ones, sq]))
            if c == 0:
                lat_norm(cq, pst_q, 4, V_QG, cqnT, 512)
        slot, sv = WS.get("kr")
        pkr = pp.next()
        fm_proj(pkr, 64, slot, sv, 0, uT, 8)
        lag()
        lat_norm(ckv, pst_k, 2, V_KVG, ckvnT, 256)
        pks = pp.next()
        fm_proj(pks, 64, slot, sv, 64, uT, 8)
        rope_apply(pkr, pks, kro[j], kro[j][0:64, :])
        for i in range(2):
            slot, sv = WS.get(f"qm1_{i}")
            for hh in range(2):
                h = 2 * i + hh
                pq = pp.next()
                fm_proj(pq, 128, slot, sv, hh * 128, uT, 8)
                A("act", lambda e, pq=pq, h=h: e.activation(out=qmT[h][:, :], in_=pq[:, 0:T], func=AF.Copy),
                  r=[pq], w=[qmT[h]])
        for i in range(2):
            slot, sv = WS.get(f"uk{i}")
            for hl in range(6):
                h = 6 * i + hl
                pb = pp.next()
                fm_proj(pb, 128, slot, sv, hl * 128, ckvnT, 2, tag="kv")
                kb = KTb[h][j]
                if hl % 2 == 0:
                    A("act", lambda e, pb=pb, kb=kb: e.activation(out=kb[:, :], in_=pb[:, 0:T], func=AF.Copy),
                      r=[pb], w=[kb])
                else:
                    A("dve", lambda e, pb=pb, kb=kb: e.tensor_copy(out=kb[:, :], in_=pb[:, 0:T]), r=[pb], w=[kb])
        for i in range(2):
            slot, sv = WS.get(f"uv{i}")
            for tc in range(NTC):
                for (c0, n) in ((0, 512), (512, 256)):
                    pb = pp.next()
                    for kc in range(2):
                        mm(pb, pb[:, 0:n], ckvnT[:, kc, tc * 128:(tc + 1) * 128], sv[:, kc, c0:c0 + n],
                           kc == 0, kc == 1, [slot, ckvnT], tag="kv")
                    A("dve", lambda e, pb=pb, tc=tc, i=i, c0=c0, n=n: e.tensor_copy(
                        out=Vb[j][:, tc, i * 768 + c0:i * 768 + c0 + n], in_=pb[:, 0:n]), r=[pb], w=[Vb[j]])

        def zgroup(i):
            slot, sv = WS.get(f"z1_{i}")
            for hh in range(2):
                pz = pp.next()
                fm_proj(pz, 128, slot, sv, hh * 128, uT, 8)
                zchunk(pz, sz[2 * i + hh])

        zgroup(6)
        zgroup(7)
        mem_attn_all(b, Sbank, podp)

        uq = {}

        def qproj(h):
            i, hh = h // 2, h % 2
            if hh == 0:
                uq["slot"], uq["sv"] = WS.get(f"uq{i}")
            slot, sv = uq["slot"], uq["sv"]
            base = hh * 256
            pqn = pp.next()
            fm_proj(pqn, 128, slot, sv, base, cqnT, 4, tag="q")
            pqr = pp.next()
            fm_proj(pqr, 64, slot, sv, base + 128, cqnT, 4, tag="q")
            pqs = pp.next()
            fm_proj(pqs, 64, slot, sv, base + 192, cqnT, 4, tag="q")
            qn = qnp.next()
            A("act", lambda e: e.activation(out=qn[:, :], in_=pqn[:, 0:T], func=AF.Copy), r=[pqn], w=[qn])
            qr = qrp.next()
            rope_apply(pqr, pqs, qr, qr[0:64, :])
            return qn, qr

        npair = j + 1
        Q = {0: qproj(0)}
        pods = {}
        steps = [(h, p) for h in range(12) for p in range(npair)]
        pSq = {}
        issued = 0
        for si, (h, p) in enumerate(steps):
            if p == 0:
                if h + 1 < 12:
                    Q[h + 1] = qproj(h + 1)
                if h % 2 == 0:
                    zgroup(h // 2)
                pods[h] = podp.next()
            while issued < len(steps) and issued <= si + 2 and steps[issued][0] <= h + 1:
                hh_, pp_ = steps[issued]
                pSq[issued] = attn_scores(hh_, pp_, *Q[hh_])
                issued += 1
            attn_step(h, p, j, pSq.pop(si), pods[h])
            if p == npair - 1:
                softmax_finish(pods[h], sz[h])
                if h == 0 and nxt_a is not None:
                    nst, nsteps = nxt_a()
                if nst is not None and nsteps:
                    nsteps.pop(0)()
                if h == 7 and nst is not None:
                    while nsteps:
                        nsteps.pop(0)()
                    nxt = nxt_b(nst)
                if h == 8 and next_bj is not None:
                    ropenext, rsteps = rope_tables_steps(*next_bj)
                if h in (8, 9, 10) and ropenext is not None:
                    rsteps.pop(0)()
        out_proj(1, xcs, dst_d, b, j, final=True, gorder=(6, 7, 0, 1, 2, 3, 4, 5))
        if ropenext is not None:
            ropecur["t"] = ropenext
            ropecur["key"] = next_bj
        return nxt

    with nc.allow_low_precision("bf16 matmul operands, fp32 accumulation"):
        prep_todo = []
        for l in layers:
            gl = weight_groups(l)
            seen = []
            for gid in mem_order(l) + tile_order(l):
                if gid not in seen:
                    seen.append(gid)
            assert set(seen) == set(gl.keys())
            prep_todo += seen
        prepped = set()

        pend = {}

        def bg_prep():
            if "fin" in pend:
                pend.pop("fin")()
                prepped.add(pend.pop("gid"))
            if prep_todo:
                gid = prep_todo.pop(0)
                pend["fin"] = prep(gid)
                pend["gid"] = gid

        def need_prepped(gid):
            while gid not in prepped:
                bg_prep()

        WS.need = need_prepped
        for li, l in enumerate(layers):
            cur["pp"] = pp_l[l]
            cur["layer"] = l
            src_d = x_d if li == 0 else h1_d
            dst_d = out_d if li == len(layers) - 1 else h1_d
            tiles = [(b, j) for b in range(nseq) for j in range(ntile)]
            dstate["built"] = 0
            dstate["total"] = len(tiles) * 12 if l == 0 else 0
            for b in range(nseq):
                mem_prep(l, b)
            pre = tile_pre_b(l, tile_pre_a(l, src_d, *tiles[0]))
            if l == 1:
                ropecur["t"] = rope_tables(*tiles[0])
                ropecur["key"] = tiles[0]
                while prep_todo or pend:
                    bg_prep()
                S.barrier()
                A("pool", lambda e: e.memset(arena[64:128, koff:koff + SEQ], 0.0), w=[kro_all])
                for qb in qr_bufs:
                    A("pool", lambda e, qb=qb: e.memset(qb[:, :], 0.0), w=[qb])
            for tix, (b, j) in enumerate(tiles):
                if tix + 1 < len(tiles):
                    nb_, nj_ = tiles[tix + 1]
                    nxt_a = (lambda l=l, src_d=src_d, nb_=nb_, nj_=nj_: tile_pre_steps(l, src_d, nb_, nj_))
                    nxt_b = (lambda st, l=l: tile_pre_b(l, st))
                else:
                    nxt_a = nxt_b = None
                if l == 0:
                    pre = l0_tile(b, j, tix, pre, nxt_a, nxt_b, dst_d)
                else:
                    pre = l1_tile(b, j, tix, pre, nxt_a, nxt_b, dst_d,
                                  tiles[tix + 1] if tix + 1 < len(tiles) else None)
        S.emit()
    return nc, []


def host_layout(inp, layers=(0, 1)):
    f = lambda a: np.ascontiguousarray(np.asarray(a, dtype=np.float32))
    col = lambda v, n: f(np.asarray(v, np.float32).reshape(n, 128).T)
    vecs = np.zeros((128, NV), np.float32)
    vecs[:, 0:8] = col(inp["norm_g"][0], 8)
    vecs[:, 8:16] = col(inp["norm_g"][1], 8)
    vecs[:, 16:24] = col(inp["mem_norm_g"][0], 8)
    vecs[:, 24:32] = col(inp["mem_norm_g"][1], 8)
    vecs[:, 32:44] = col(inp["conv_dw_b"][0], 12)
    vecs[:, 44:56] = col(inp["conv_ln_g"][0], 12)
    vecs[:, 56:68] = col(inp["conv_ln_b"][0], 12)
    vecs[:, 68:72] = col(inp["mla_q_norm_g"][0], 4)
    vecs[:, 72:74] = col(inp["mla_kv_norm_g"][0], 2)
    invf = (1.0 / (np.float32(10000.0) ** (np.arange(0, 64, 2, dtype=np.float32) / np.float32(64)))).astype(np.float32)
    vecs[0:64, 74] = np.concatenate([invf, invf])
    vecs[0:64, 75] = np.concatenate([-np.ones(32, np.float32), np.ones(32, np.float32)])
    dw = np.asarray(inp["conv_dw"][0], np.float32)
    vecs[:, 76:76 + 372] = dw.reshape(31, 12, 128).transpose(2, 1, 0).reshape(128, 372)
    cst = np.zeros((128, 128 + 2 * T), np.float32)
    cst[:, 0:128] = np.eye(128, dtype=np.float32)
    p = np.arange(128)[:, None]
    q = np.arange(T)[None, :]
    for r in range(2):
        cst[:, 128 + r * T:128 + (r + 1) * T] = (q >= r * 128 + p).astype(np.float32)
    shared = {"vecs": vecs, "cst": cst, "fg": f(inp["final_norm_g"]).reshape(1, D),
              "wout": f(inp["w_out"]), "wmem": f(inp["w_mem_kv"])}
    if 0 in layers:
        w = np.asarray(inp["conv_w_in"][0], np.float32)
        a, g_, qm, z = w[:, 0:1536], w[:, 1536:3072], w[:, 3072:3584], w[:, 3584:5632]
        ag = np.stack([a.reshape(1024, 12, 128), g_.reshape(1024, 12, 128)], axis=2).reshape(1024, 3072)
        shared["w0"] = f(np.concatenate([ag, z, qm], axis=1))
    if 1 in layers:
        w = np.asarray(inp["mla_w_in"][0], np.float32)
        kr = w[:, 768:832]
        kr_sw = np.concatenate([kr[:, 32:64], kr[:, 0:32]], axis=1)
        shared["w1"] = f(np.concatenate([w[:, 0:768], kr, kr_sw, w[:, 832:1344], w[:, 1344:3392]], axis=1))
        uq = np.asarray(inp["mla_w_uq"][0], np.float32).reshape(512, 12, 192)
        qn, qr = uq[:, :, 0:128], uq[:, :, 128:192]
        qsw = np.concatenate([qr[:, :, 32:64], qr[:, :, 0:32]], axis=2)
        shared["wuq"] = f(np.concatenate([qn, qr, qsw], axis=2).reshape(512, 3072))
        ukv = np.asarray(inp["mla_w_ukv"][0], np.float32).reshape(256, 12, 256)
        shared["wuk"] = f(ukv[:, :, 0:128].reshape(256, 1536))
        shared["wuv"] = f(ukv[:, :, 128:256].reshape(256, 1536))
    return shared


_PROG_CACHE = {}


def _get_prog(layers):
    if layers not in _PROG_CACHE:
        _PROG_CACHE[layers] = build_program(layers)[0]
    return _PROG_CACHE[layers]


def run_layers(inp, xin, layers):
    shared = host_layout(inp, layers)
    mem = np.asarray(inp["mem"], np.float32)
    pos = np.asarray(inp["positions"], np.int32)
    nc = _get_prog(layers)
    in_maps = []
    for c in range(NCORES):
        m = dict(shared)
        m["x"] = np.ascontiguousarray(xin[c * NB:(c + 1) * NB])
        m["mem"] = np.ascontiguousarray(mem[c * NB:(c + 1) * NB])
        m["pos"] = np.ascontiguousarray(pos[c * NB:(c + 1) * NB])
        in_maps.append(m)
    res = run_bass_kernel_spmd(nc, in_maps, core_ids=list(range(NCORES)))
    return np.concatenate([r["out"] for r in res.results], axis=0)


FUSED = True


def kernel(**inp):
    x = np.asarray(inp["x"], np.float32)
    if FUSED:
        return run_layers(inp, x, (0, 1)).astype(np.float32)
    h1 = run_layers(inp, x, (0,))
    return run_layers(inp, h1, (1,)).astype(np.float32)
```
